# Optimizing a Trainium2 kernel written in Bass

```python
import jax, jax.numpy as jnp
from jax import lax
import numpy as np

D_MODEL = 1024
BATCH = 2
SEQ = 8192
DEPTH = 4

GRID_W = 64
D_MIX = 2 * D_MODEL
POOL_WIDTH = D_MIX // 4
LRU_WIDTH = 3 * D_MIX // 8
ATTN_WIDTH = 3 * D_MIX // 8
POOL_WINDOWS = (2, 4, 8, 16)
POOL_GROUPS = len(POOL_WINDOWS)
POOL_GROUP_DIM = POOL_WIDTH // POOL_GROUPS
LRU_BLOCKS = 6
LRU_BLOCK_DIM = LRU_WIDTH // LRU_BLOCKS
LRU_C = 8.0
CONV_WIDTH = 4
CONV_PAD_LEFT = 1
HEAD_DIM = 128
N_Q_HEADS = ATTN_WIDTH // HEAD_DIM
N_KV_HEADS = 2
GQA_GROUP = N_Q_HEADS // N_KV_HEADS
KV_WIDTH = N_KV_HEADS * HEAD_DIM
ROPE_AXIS_DIM = HEAD_DIM // 2
ROPE_BASE = 10000.0
Q_BLOCK = 128
EPS = 1e-6
IN_SIZES = (POOL_WIDTH, POOL_WIDTH, LRU_WIDTH, LRU_WIDTH, ATTN_WIDTH, KV_WIDTH, KV_WIDTH, ATTN_WIDTH)
D_IN = sum(IN_SIZES)

kernel_name = "hymba_pool_rglru_axialgqa_encoder"


def rmsnorm(x, g):
    xf = x.astype(jnp.float32)
    y = xf * lax.rsqrt(jnp.mean(xf * xf, axis=-1, keepdims=True) + EPS)
    return (y * g.astype(jnp.float32)).astype(x.dtype)


def pool_mixer(u, w_pool, scale):
    B, L, _ = u.shape
    uf = u.astype(jnp.float32)
    cs = jnp.concatenate([jnp.zeros((B, 1, POOL_WIDTH), jnp.float32), jnp.cumsum(uf, axis=1)], axis=1)
    t = jnp.arange(L)
    means = []
    for g, w in enumerate(POOL_WINDOWS):
        half = w // 2
        lo = jnp.clip(t - half, 0, L)
        hi = jnp.clip(t + half, 0, L)
        csg = cs[..., g * POOL_GROUP_DIM:(g + 1) * POOL_GROUP_DIM]
        cnt = (hi - lo).astype(jnp.float32)[None, :, None]
        means.append((csg[:, hi] - csg[:, lo]) / cnt)
    pooled = jnp.stack(means, axis=2) - uf.reshape(B, L, POOL_GROUPS, POOL_GROUP_DIM)
    mixed = jnp.einsum('blgc,gcd->blgd', pooled.astype(u.dtype), w_pool)
    return mixed.reshape(B, L, POOL_WIDTH) * scale


def short_conv(u, w, b):
    L = u.shape[1]
    up = jnp.pad(u, ((0, 0), (CONV_PAD_LEFT, CONV_WIDTH - 1 - CONV_PAD_LEFT), (0, 0)))
    y = b
    for k in range(CONV_WIDTH):
        y = y + up[:, k:k + L] * w[k]
    return y


def _lin_combine(earlier, later):
    a1, b1 = earlier
    a2, b2 = later
    return (a1 * a2, a2 * b1 + b2)


def rg_lru_bidir(u, conv_w, conv_b, w_r, b_r, w_i, b_i, lam):
    B, L, _ = u.shape
    xc = short_conv(u, conv_w, conv_b)
    xb = xc.reshape(B, L, LRU_BLOCKS, LRU_BLOCK_DIM)
    xf = xc.astype(jnp.float32)
    hs = []
    for d, rev in enumerate((False, True)):
        r = jax.nn.sigmoid((jnp.einsum('blhc,hcd->blhd', xb, w_r[d]).reshape(B, L, LRU_WIDTH) + b_r[d]).astype(jnp.float32))
        i = jax.nn.sigmoid((jnp.einsum('blhc,hcd->blhd', xb, w_i[d]).reshape(B, L, LRU_WIDTH) + b_i[d]).astype(jnp.float32))
        log_a = LRU_C * r * jax.nn.log_sigmoid(lam[d].astype(jnp.float32))
        a = jnp.exp(log_a)
        inp = jnp.sqrt(-jnp.expm1(2.0 * log_a)) * (i * xf)
        _, h = lax.associative_scan(_lin_combine, (a, inp), reverse=rev, axis=1)
        hs.append(h)
    return (hs[0] + hs[1]).astype(u.dtype)


def axial_rope_tables(L):
    rows = L // GRID_W
    row = jnp.repeat(jnp.arange(rows), GRID_W).astype(jnp.float32)
    col = jnp.tile(jnp.arange(GRID_W), rows).astype(jnp.float32)
    inv = ROPE_BASE ** (-jnp.arange(0, ROPE_AXIS_DIM, 2, dtype=jnp.float32) / ROPE_AXIS_DIM)
    ang = jnp.stack([row[:, None] * inv, col[:, None] * inv], axis=1)
    return jnp.cos(ang), jnp.sin(ang)


def apply_axial_rope(x, cos, sin):
    B, L, H, D = x.shape
    xs = x.astype(jnp.float32).reshape(B, L, H, 2, 2, ROPE_AXIS_DIM // 2)
    x1, x2 = xs[..., 0, :], xs[..., 1, :]
    c, s = cos[:, None], sin[:, None]
    out = jnp.stack([x1 * c - x2 * s, x2 * c + x1 * s], axis=-2)
    return out.reshape(B, L, H, D).astype(x.dtype)


def block_attention(q, k, v):
    B, L = q.shape[0], q.shape[1]
    nb = L // Q_BLOCK
    qb = q.reshape(B, nb, Q_BLOCK, N_KV_HEADS, GQA_GROUP, HEAD_DIM).transpose(1, 0, 2, 3, 4, 5)
    scale = HEAD_DIM ** -0.5

    def one_block(qblk):
        s = jnp.einsum('bqkgd,bskd->bkgqs', qblk, k, preferred_element_type=jnp.float32) * scale
        p = jax.nn.softmax(s, axis=-1)
        return jnp.einsum('bkgqs,bskd->bqkgd', p.astype(v.dtype), v)

    o = lax.map(one_block, qb)
    return o.transpose(1, 0, 2, 3, 4, 5).reshape(B, L, ATTN_WIDTH)


def setup_inputs(seed: int = 0) -> dict:
    key = jax.random.key(seed)
    ks = jax.random.split(key, 16)
    f32 = jnp.float32
    nrm = lambda k, shape, s: jax.random.normal(k, shape, f32) * s
    u = jax.random.uniform(ks[11], (DEPTH, 2, LRU_WIDTH), f32, 0.9, 0.999)
    sg = u ** (1.0 / LRU_C)
    lru_lam = jnp.log(sg) - jnp.log1p(-sg)
    return {
        "x": nrm(ks[0], (BATCH, SEQ, D_MODEL), 1.0),
        "norm_g": 1.0 + nrm(ks[1], (DEPTH, D_MODEL), 0.02),
        "w_in": nrm(ks[2], (DEPTH, D_MODEL, D_IN), D_MODEL ** -0.5),
        "pool_w": nrm(ks[3], (DEPTH, POOL_GROUPS, POOL_GROUP_DIM, POOL_GROUP_DIM), POOL_GROUP_DIM ** -0.5),
        "pool_scale": 1.0 + nrm(ks[4], (DEPTH, POOL_WIDTH), 0.02),
        "conv_w": nrm(ks[5], (DEPTH, CONV_WIDTH, LRU_WIDTH), CONV_WIDTH ** -0.5),
        "conv_b": nrm(ks[6], (DEPTH, LRU_WIDTH), 0.01),
        "lru_wr": nrm(ks[7], (DEPTH, 2, LRU_BLOCKS, LRU_BLOCK_DIM, LRU_BLOCK_DIM), LRU_BLOCK_DIM ** -0.5),
        "lru_br": nrm(ks[8], (DEPTH, 2, LRU_WIDTH), 0.01),
        "lru_wi": nrm(ks[9], (DEPTH, 2, LRU_BLOCKS, LRU_BLOCK_DIM, LRU_BLOCK_DIM), LRU_BLOCK_DIM ** -0.5),
        "lru_bi": nrm(ks[10], (DEPTH, 2, LRU_WIDTH), 0.01),
        "lru_lam": lru_lam,
        "q_norm": 1.0 + nrm(ks[12], (DEPTH, HEAD_DIM), 0.02),
        "k_norm": 1.0 + nrm(ks[13], (DEPTH, HEAD_DIM), 0.02),
        "w_out": nrm(ks[14], (DEPTH, D_MIX, D_MODEL), D_MIX ** -0.5),
    }


def reference(x, norm_g, w_in, pool_w, pool_scale, conv_w, conv_b, lru_wr, lru_br, lru_wi, lru_bi,
              lru_lam, q_norm, k_norm, w_out):
    B, L, _ = x.shape
    cos, sin = axial_rope_tables(L)
    splits = []
    acc = 0
    for s in IN_SIZES[:-1]:
        acc += s
        splits.append(acc)
    for l in range(DEPTH):
        h = rmsnorm(x, norm_g[l])
        z = jnp.einsum('bld,de->ble', h, w_in[l])
        u_pool, g_pool, u_lru, g_lru, q, k, v, g_attn = jnp.split(z, splits, axis=-1)

        y_pool = pool_mixer(u_pool, pool_w[l], pool_scale[l]) * jax.nn.silu(g_pool)

        y_lru = rg_lru_bidir(u_lru, conv_w[l], conv_b[l], lru_wr[l], lru_br[l], lru_wi[l], lru_bi[l],
                             lru_lam[l]) * jax.nn.silu(g_lru)

        q = rmsnorm(q.reshape(B, L, N_Q_HEADS, HEAD_DIM), q_norm[l])
        k = rmsnorm(k.reshape(B, L, N_KV_HEADS, HEAD_DIM), k_norm[l])
        v = v.reshape(B, L, N_KV_HEADS, HEAD_DIM)
        q = apply_axial_rope(q, cos, sin)
        k = apply_axial_rope(k, cos, sin)
        y_attn = block_attention(q, k, v) * jax.nn.silu(g_attn)

        y = jnp.concatenate([y_pool, y_lru, y_attn], axis=-1)
        x = x + jnp.einsum('ble,ed->bld', y, w_out[l])
    return x
```

```python
import contextlib
import numpy as np
import concourse.bass as bass
import concourse.mybir as mybir
from concourse.bass_utils import run_bass_kernel_spmd

F32 = mybir.dt.float32
BF16 = mybir.dt.bfloat16
AF = mybir.ActivationFunctionType
ALU = mybir.AluOpType
AX = mybir.AxisListType

ENGS = ("pe", "act", "dve", "pool", "sp")
NCORES = 8
DEPTH = 4
NT = 2048
TB = 512
NTB = NT // TB
HALO = 8
EPS = 1e-6
NV = 78
RG = [[0, 1, 2, 3], [4, 5, 6, 7]]
POOL_W = (2, 4, 8, 16)


class _Stop(Exception):
    pass


class Prog:
    def __init__(self, nc, stack):
        self.nc = nc
        self.stack = stack
        self.streams = {e: [] for e in ENGS}
        self.esem = {e: stack.enter_context(nc.semaphore("s_" + e)) for e in ENGS if e != "sp"}
        self.ecnt = {e: 0 for e in ENGS}
        self.dsem = {}
        self.dcnt = {}
        self.last_w = {}
        self.readers = {}
        self.waited = {e: {} for e in ENGS}
        self.nops = 0

    def _dma_sem(self, name):
        if name not in self.dsem:
            self.dsem[name] = self.stack.enter_context(self.nc.semaphore("d_" + name))
            self.dcnt[name] = 0
        return self.dsem[name]

    def _need(self, eng, reads, writes, rawonly=()):
        need = {}

        def add(ev):
            if ev is None:
                return
            sem, val, src = ev
            if src == "pe" and eng == "pe":
                return
            k = id(sem)
            if k not in need or need[k][1] < val:
                need[k] = (sem, val)

        for k in reads:
            add(self.last_w.get(k))
        for k in rawonly:
            add(self.last_w.get(k))
        for k in writes:
            add(self.last_w.get(k))
            for ev in self.readers.get(k, {}).values():
                add(ev)
        out = []
        for k, (sem, val) in need.items():
            if self.waited[eng].get(k, 0) < val:
                self.waited[eng][k] = val
                out.append((sem, val))
        return out

    def _record(self, ev, reads, writes):
        for k in writes:
            self.last_w[k] = ev
            self.readers[k] = {}
        for k in reads:
            d = self.readers.setdefault(k, {})
            kk = id(ev[0])
            if kk not in d or d[kk][1] < ev[1]:
                d[kk] = ev

    def op(self, eng, fn, reads=(), writes=(), rawonly=()):
        waits = self._need(eng, reads, writes, rawonly)
        self.ecnt[eng] += 1
        sem = self.esem[eng]
        self.streams[eng].append((waits, fn, sem, 1))
        self.nops += 1
        self._record((sem, self.ecnt[eng], eng), reads, writes)

    def dma(self, q, group, fn, reads=(), writes=(), inc=16):
        waits = self._need(q, reads, writes)
        sem = self._dma_sem(group)
        self.dcnt[group] += inc
        self.streams[q].append((waits, fn, sem, inc))
        self.nops += 1
        self._record((sem, self.dcnt[group], "dma"), reads, writes)

    def final_wait(self, eng, keys):
        waits = self._need(eng, keys, ())
        self.streams[eng].append((waits, None, None, 0))

    def emit(self):
        with self.nc.Block() as block:
            def replay(name):
                def body(eng):
                    for waits, fn, sem, inc in self.streams[name]:
                        for (s, v) in waits:
                            eng.wait_ge(s, v)
                        if fn is not None:
                            fn(eng).then_inc(sem, inc)
                return body
            block.tensor(replay("pe"))
            block.scalar(replay("act"))
            block.vector(replay("dve"))
            block.gpsimd(replay("pool"))
            block.sync(replay("sp"))


def rsl(lo, hi):
    return slice(hi - 1, lo - 1 if lo > 0 else None, -1)


def run_interleaved(factories, nsets, bg=None):
    free = list(range(nsets))
    active = []
    it = iter(factories)

    def fill():
        while free:
            try:
                f = next(it)
            except StopIteration:
                return
            k = free.pop(0)
            active.append((f(k), k))

    fill()
    while active:
        for g, k in list(active):
            try:
                next(g)
            except StopIteration:
                active.remove((g, k))
                free.append(k)
                fill()
        if bg is not None:
            try:
                next(bg)
            except StopIteration:
                bg = None
    if bg is not None:
        for _ in bg:
            pass


def build(nl):
    nc = bass.Bass("TRN2", target_bir_lowering=False)
    dI = lambda name, shape, dt=F32: nc.dram_tensor(name, shape, dt, kind="ExternalInput").ap()
    xT_d = dI("xT", [1024, NT])
    xh_d = dI("xh", [1024, 2 * HALO])
    cs_d = dI("cs", [NT, 128])
    cst_d = dI("cst", [128, 80])
    vecs_d = dI("vecs", [nl, 128, NV])
    qkn_d = dI("qkn", [nl, 1, 256])
    w_in_d = dI("w_in", [nl, 1024, 4608])
    w_out_d = dI("w_out", [nl, 2048, 1024])
    pool_w_d = dI("pool_w", [nl, 4, 128, 128])
    wr_d = dI("lru_wr", [nl, 2, 6, 128, 128])
    wi_d = dI("lru_wi", [nl, 2, 6, 128, 128])
    yT_d = nc.dram_tensor("yT", [1024, NT], F32, kind="ExternalOutput").ap()
    kb_d = nc.dram_tensor("kb", [256, NT], BF16).ap()
    kg_d = nc.dram_tensor("kg", [4 * 256, NT], BF16).ap()
    vb_d = nc.dram_tensor("vb", [NT, 256], BF16).ap()
    vg_d = nc.dram_tensor("vg", [4 * NT, 256], BF16).ap()
    sb_d = nc.dram_tensor("sbn", [128, 24], F32).ap()
    sg_d = nc.dram_tensor("sgt", [4 * 128, 24], F32).ap()
    cb_d = nc.dram_tensor("cbs", [128, 6, NT], BF16).ap()
    xe_d = nc.dram_tensor("xe", [1024, 2 * HALO], F32).ap()
    xg_d = nc.dram_tensor("xg", [4 * 1024, 2 * HALO], F32).ap()

    with contextlib.ExitStack() as st:
        P = Prog(nc, st)
        sbt = lambda name, shape, dt=F32: st.enter_context(nc.sbuf_tensor(name, shape, dt))
        X = sbt("X", [128, 8, NT])
        HX = sbt("HX", [128, 8, NT + 2 * HALO], BF16)
        BIG = sbt("BIG", [128, 16384], BF16)
        WB = sbt("WB", [128, 18432], BF16)
        XH = sbt("XH", [128, 8, 2 * HALO])
        CSB = [sbt("CSB%d" % i, [128, 128]) for i in range(2)]
        CST = sbt("CST", [128, 80])
        VEC = sbt("VEC", [128, NV])
        QKG = sbt("QKG", [128, 256])
        NB = sbt("NB", [128, 24])
        LSL = sbt("LSL", [128, 12])
        LSL2 = sbt("LSL2", [128, 12])
        LT = [sbt("LT%d" % i, [128, 12]) for i in range(4)]
        IDB = sbt("IDB", [128, 128], BF16)
        ONF = sbt("ONF", [128, 128])
        ONB = sbt("ONB", [128, 128], BF16)
        ZC = sbt("ZC", [128, 1])
        T = [sbt("T%d" % i, [128, 528]) for i in range(14)]
        XCB = [sbt("XCB%d" % i, [128, TB], BF16) for i in range(2)]
        RSH = sbt("RSH", [128, 2 * HALO])
        YB = sbt("YB", [128, 6, TB], BF16)
        SS = [sbt("SS%d" % i, [128, 4]) for i in range(2)]
        RS = [sbt("RS%d" % i, [128, 4]) for i in range(2)]
        HS = sbt("HS", [128, 24])
        SG4 = sbt("SG4", [128, 4, 24])
        EB = sbt("EB", [128, 4, 6])
        PB = sbt("PB", [128, 4, 6])
        HIN = sbt("HIN", [128, 4, 6])
        CARB = sbt("CARB", [128, 6])
        CAR = [sbt("CAR%d" % i, [128, 6]) for i in range(5)]
        CARRY = sbt("CARRY", [128, 6])
        IDF = T[1][:, 0:128]
        PT = []
        PTT = (7, 10, 11)
        for ti in PTT:
            v = T[ti][:, 0:TB].bitcast(BF16)
            PT += [v[:, 0:TB], v[:, TB:2 * TB]]
        NPT = len(PT)
        ptk = lambda pb: tk(PTT[pb // 2])
        PS = [st.enter_context(nc.psum_tensor("ps%d" % i, [128, TB], F32)) for i in range(6)]
        PST = [st.enter_context(nc.psum_tensor("pst%d" % i, [128, 1024], BF16)) for i in range(2)]

        tk = lambda i: "T%d" % i
        KLT = BIG[:, 0:4096].rearrange("p (h t) -> p h t", h=2)
        VL = BIG[:, 4096:8192].rearrange("p (n f) -> p n f", n=16)
        CF = BIG[:, 0:12288].rearrange("p (j t) -> p j t", j=6)
        KT = BIG[:, 0:8192].rearrange("p (r t) -> p r t", r=4)
        VA = BIG[:, 8192:16384].rearrange("p (n f) -> p n f", n=64)
        kKLT, kVL, kCF, kKT, kVA, kWRI = ["bA"], ["bB"], ["bA", "bB", "bC"], ["bA", "bB"], ["bC", "bD"], ["bD"]
        XG = T[0][:, 0:512].rearrange("p (r c t) -> p r c t", r=4, c=8)

        def wview(off, c, n):
            return WB[:, off:off + c * n].rearrange("p (c n) -> p c n", c=c)

        def TT(eng, out, in0, in1, op, r, w):
            P.op(eng, lambda e: e.tensor_tensor(out=out, in0=in0, in1=in1, op=op), r, w)

        def TS(eng, out, in0, s1, s2, op0, op1, r, w):
            P.op(eng, lambda e: e.tensor_scalar(out=out, in0=in0, scalar1=s1, scalar2=s2, op0=op0, op1=op1), r, w)

        def STT(out, in0, scalar, in1, op0, op1, r, w):
            P.op("dve", lambda e: e.scalar_tensor_tensor(out=out, in0=in0, scalar=scalar, in1=in1, op0=op0, op1=op1), r, w)

        def ACT(out, in_, func, r, w, **kw):
            P.op("act", lambda e: e.activation(out=out, in_=in_, func=func, **kw), r, w)

        def MM(out, lhsT, rhs, start, stop, r, w, rawonly=()):
            P.op("pe", lambda e: e.matmul(out, lhsT=lhsT, rhs=rhs, start=start, stop=stop), r, w, rawonly)

        def TR(out, in_, r, w):
            P.op("pe", lambda e: e.transpose(out=out, in_=in_, identity=IDB[:]), list(r) + ["IDB"], w)

        def RECIP(out, in_, r, w):
            P.op("dve", lambda e: e.reciprocal(out=out, in_=in_), r, w)

        def SCAN(out, d0, d1, init, r, w):
            P.op("dve", lambda e: e.tensor_tensor_scan(out=out, data0=d0, data1=d1, initial=init, op0=ALU.mult, op1=ALU.add), r, w)

        def DMA(q, group, out, in_, r, w):
            P.dma(q, group, lambda e: e.dma_start(out=out, in_=in_), r, w)

        def AG(group, in_, out, r, w):
            P.dma("pool", group, lambda e: e.collective_compute("AllGather", ALU.bypass, replica_groups=RG, ins=[in_.opt()], outs=[out.opt()]), r, w, inc=1)

        def sigmoid_from(out, ok, src, sk, **kw):
            ACT(out, src, AF.Exp, [sk], [ok], **kw)
            ACT(out, out, AF.Ln, [ok], [ok], bias=1.0)
            ACT(out, out, AF.Exp, [ok], [ok], scale=-1.0)

        def outproj_add(wo, wk, nj, ykeys, tb, rev=False, banks=(4, 5)):
            xsl = slice(tb * TB, (tb + 1) * TB)
            for m in range(8):
                b = banks[m % len(banks)]
                for j in range(nj):
                    MM(PS[b][:], wo[:, j, m * 128:(m + 1) * 128], YB[:, j, :], j == 0, j == nj - 1, [wk] + ykeys, [("ps", b)])
                src = PS[b][:, ::-1] if rev else PS[b][:]
                TT("dve", X[:, m, xsl], X[:, m, xsl], src, ALU.add, [("ps", b), ("X", m)], [("X", m)])

        cs_state = {"seq": [], "i": 0}

        def cs_begin(seq):
            cs_state["seq"] = list(seq)
            cs_state["i"] = 0
            for i in range(min(2, len(seq))):
                cs_load(i)

        def cs_load(i):
            n = cs_state["seq"][i]
            DMA("sp", "CSB%d" % (i % 2), CSB[i % 2][:], cs_d[n * 128:(n + 1) * 128, :], [], [("CSB", i % 2)])

        def cs_use():
            i = cs_state["i"]
            cs_state["i"] += 1
            return i

        def cs_done(i):
            if i + 2 < len(cs_state["seq"]):
                cs_load(i + 2)

        def normrope_gen(src, srck, H, gain, dstb, dstk, k):
            n = H * 128
            t0, t1, t2, qn = T[6 * k], T[6 * k + 1], T[6 * k + 2], T[6 * k + 3]
            k0, k1, k2, kq = tk(6 * k), tk(6 * k + 1), tk(6 * k + 2), tk(6 * k + 3)
            ci = cs_use()
            csb, csk = CSB[ci % 2], ("CSB", ci % 2)
            ACT(t0[:, 0:n], src, AF.Square, [srck], [k0])
            ACT(qn[:, 0:n], src, AF.Copy, [srck], [kq])
            yield
            P.op("dve", lambda e: e.tensor_reduce(out=SS[k][:, 0:H], in_=t0[:, 0:n].rearrange("p (h d) -> p h d", h=H), axis=AX.X, op=ALU.add), [k0], [("SS", k)])
            ACT(SS[k][:, 0:H], SS[k][:, 0:H], AF.Ln, [("SS", k)], [("SS", k)], scale=1.0 / 128, bias=EPS)
            ACT(RS[k][:, 0:H], SS[k][:, 0:H], AF.Exp, [("SS", k)], [("RS", k)], scale=-0.5)
            yield
            for h in range(H):
                STT(qn[:, h * 128:(h + 1) * 128], qn[:, h * 128:(h + 1) * 128], RS[k][:, h:h + 1], gain, ALU.mult, ALU.mult, [kq, ("RS", k), "QKG"], [kq])
            yield
            q5 = qn[:, 0:n].rearrange("p (h a b e) -> p h a b e", h=H, a=2, b=2)
            d5 = dstb.rearrange("p (h a b e) -> p h a b e", h=H, a=2, b=2)
            x1, x2 = q5[:, :, :, 0, :], q5[:, :, :, 1, :]
            cosb = csb[:, 0:64].rearrange("p (a e) -> p a e", a=2).unsqueeze(1).to_broadcast([128, H, 2, 32])
            sinb = csb[:, 64:128].rearrange("p (a e) -> p a e", a=2).unsqueeze(1).to_broadcast([128, H, 2, 32])
            hv = lambda t: t[:, 0:H * 64].rearrange("p (h a e) -> p h a e", h=H, a=2)
            TT("dve", hv(t1), x1, cosb, ALU.mult, [kq, csk], [k1])
            TT("dve", hv(t2), x2, sinb, ALU.mult, [kq, csk], [k2])
            TT("dve", d5[:, :, :, 0, :], hv(t1), hv(t2), ALU.subtract, [k1, k2], [dstk])
            yield
            TT("dve", hv(t1), x2, cosb, ALU.mult, [kq, csk], [k1])
            TT("dve", hv(t2), x1, sinb, ALU.mult, [kq, csk], [k2])
            TT("dve", d5[:, :, :, 1, :], hv(t1), hv(t2), ALU.add, [k1, k2], [dstk])
            cs_done(ci)
            yield

        P.op("pool", lambda e: e.memset(IDF, 0.0), [], [tk(1)])
        P.op("pool", lambda e: e.affine_select(out=IDF, in_=IDF, pattern=[[-1, 128]], compare_op=ALU.not_equal, fill=1.0, base=0, channel_multiplier=1), [tk(1)], [tk(1)])
        P.op("pool", lambda e: e.tensor_copy(out=IDB[:], in_=IDF), [tk(1)], ["IDB"])
        P.op("pool", lambda e: e.memset(ONF[:], 1.0), [], ["ONF"])
        P.op("pool", lambda e: e.memset(ONB[:], 1.0), [], ["ONB"])
        P.op("pool", lambda e: e.memset(ZC[:], 0.0), [], ["ZC"])
        DMA("sp", "X", X[:], xT_d.rearrange("(c p) t -> p c t", p=128), [], [("X", m) for m in range(8)])
        DMA("sp", "XH", XH[:], xh_d.rearrange("(c p) t -> p c t", p=128), [], ["XH"])
        DMA("sp", "CST", CST[:], cst_d, [], ["CST"])

        def layer(l):
            if l > 0:
                xk = [("X", m) for m in range(8)]
                DMA("sp", "xe", xe_d.rearrange("(c p) t -> p c t", p=128)[:, :, 0:HALO], X[:, :, 0:HALO], xk, ["xe"])
                DMA("sp", "xe", xe_d.rearrange("(c p) t -> p c t", p=128)[:, :, HALO:2 * HALO], X[:, :, NT - HALO:NT], xk, ["xe"])
                AG("agx", xe_d, xg_d, ["xe"], ["xg"])
            DMA("sp", "VEC", VEC[:], vecs_d[l], [], ["VEC"])
            DMA("sp", "QKG", QKG[:], qkn_d[l].partition_broadcast(128), [], ["QKG"])
            TS("dve", NB[:], VEC[:, 42:66], -1.0, None, ALU.mult, ALU.bypass, ["VEC"], ["NB"])
            lam = VEC[:, 66:78]
            TS("dve", LT[1][:], lam, -1.0, None, ALU.mult, ALU.bypass, ["VEC"], ["LT1"])
            TT("dve", LT[0][:], LT[1][:], lam, ALU.max, ["LT1", "VEC"], ["LT0"])
            ACT(LT[0][:], LT[0][:], AF.Exp, ["LT0"], ["LT0"], scale=-1.0)
            TS("dve", LT[1][:], LT[0][:], 2.0, None, ALU.add, ALU.bypass, ["LT0"], ["LT1"])
            RECIP(LT[1][:], LT[1][:], ["LT1"], ["LT1"])
            TT("dve", LT[1][:], LT[1][:], LT[0][:], ALU.mult, ["LT1", "LT0"], ["LT1"])
            TT("dve", LT[2][:], LT[1][:], LT[1][:], ALU.mult, ["LT1"], ["LT2"])
            TS("dve", LT[3][:], LT[2][:], 1.0 / 13, 1.0 / 11, ALU.mult, ALU.add, ["LT2"], ["LT3"])
            for cf in (1.0 / 9, 1.0 / 7, 1.0 / 5, 1.0 / 3, 1.0):
                TT("dve", LT[3][:], LT[3][:], LT[2][:], ALU.mult, ["LT3", "LT2"], ["LT3"])
                TS("dve", LT[3][:], LT[3][:], cf, None, ALU.add, ALU.bypass, ["LT3"], ["LT3"])
            TT("dve", LT[3][:], LT[3][:], LT[1][:], ALU.mult, ["LT3", "LT1"], ["LT3"])
            TS("dve", LT[0][:], lam, 0.0, 8.0, ALU.min, ALU.mult, ["VEC", "LT0"], ["LT0"])
            STT(LSL[:], LT[3][:], -16.0, LT[0][:], ALU.mult, ALU.add, ["LT3", "LT0"], ["LSL"])
            TS("dve", LSL2[:], LSL[:], 2.0, None, ALU.mult, ALU.bypass, ["LSL"], ["LSL2"])

            def rmsnorm_gen(tb):
                s = tb * TB
                rstd, rk = T[12 + tb % 2][:, 0:TB], tk(12 + tb % 2)
                psb = 4 + tb % 2
                for c in range(8):
                    t = T[4 + c % 2]
                    ACT(t[:, 0:TB], X[:, c, s:s + TB], AF.Square, [("X", c)], [tk(4 + c % 2)])
                    MM(PS[psb][:], ONF[:], t[:, 0:TB], c == 0, c == 7, ["ONF", tk(4 + c % 2)], [("ps", psb)])
                    if c % 2 == 1:
                        yield
                ACT(rstd, PS[psb][:], AF.Ln, [("ps", psb)], [rk], scale=1.0 / 1024, bias=EPS)
                ACT(rstd, rstd, AF.Exp, [rk], [rk], scale=-0.5)
                yield
                for c in range(8):
                    STT(HX[:, c, HALO + s:HALO + s + TB], X[:, c, s:s + TB], VEC[:, c:c + 1], rstd, ALU.mult, ALU.mult,
                        [("X", c), "VEC", rk], ["HX", "HXa"] if tb == 0 else ["HX"])
                    if c % 2 == 1:
                        yield

            for _ in rmsnorm_gen(0):
                pass

            if STOP == 'norm':
                raise _Stop()
            Wkv = wview(0, 8, 512)
            DMA("pool", "WBk", Wkv, w_in_d[l, :, 3328:3840].rearrange("(c p) e -> p c e", p=128), [], ["WBk", "WB", "WBa0", "WBa1"])
            Wl = wview(0, 8, 1536)
            WR = BIG[:, 12288:13824].rearrange("p (d j e) -> p d j e", d=2, j=6)
            WI = BIG[:, 13824:15360].rearrange("p (d j e) -> p d j e", d=2, j=6)
            WoL = WB[:, 12288:18432].rearrange("p (j m) -> p j m", j=6)
            DMA("pool", "WB", Wl[:, 3:8, :], w_in_d[l, 384:1024, 1024:2560].rearrange("(c p) e -> p c e", p=128), [], ["WB", "WBa0", "WBa1"])
            for d in range(2):
                DMA("pool", "WRI", WR[:, d], wr_d[l, d].rearrange("j c e -> c j e"), [], kWRI)
                DMA("pool", "WRI", WI[:, d], wi_d[l, d].rearrange("j c e -> c j e"), [], kWRI)
            DMA("pool", "WB", WoL, w_out_d[l, 512:1280, :].rearrange("(j p) m -> p j m", p=128), [], ["WB"])
            cs_begin(list(range(16)))

            def ktile(n):
                def gen(k):
                    pb = k
                    for c in range(8):
                        MM(PS[pb][:], HX[:, c, HALO + n * 128:HALO + (n + 1) * 128], Wkv[:, c, :], c == 0, c == 7, ["HXa", "WBk"], [("ps", pb)], rawonly=["HX"])
                    ACT(VL[:, n, :], PS[pb][:, 256:512], AF.Copy, [("ps", pb)], kVL)
                    yield from normrope_gen(PS[pb][:, 0:256], ("ps", pb), 2, QKG[:, 128:256], XCB[k][:, 0:256], ("XCB", k), k)
                    pk = ("pst", k)
                    for h in range(2):
                        TR(PST[k][:, h * 128:(h + 1) * 128], XCB[k][:, h * 128:(h + 1) * 128], [("XCB", k)], [pk])
                    P.op("act", lambda e: e.activation(out=KLT[:, :, n * 128:(n + 1) * 128], in_=PST[k][:, 0:256].rearrange("p (h t) -> p h t", h=2), func=AF.Copy), [pk], kKLT)
                    yield
                return gen

            for tb in range(NTB):
                run_interleaved([ktile(n) for n in range(4 * tb, 4 * tb + 4)], KT_SETS,
                                bg=rmsnorm_gen(tb + 1) if tb + 1 < NTB else None)
            if STOP == 'A0':
                raise _Stop()
            DMA("sp", "kb", kb_d.rearrange("(h p) t -> p h t", p=128), KLT, kKLT, ["kb"])
            DMA("sp", "vb", vb_d.rearrange("(n p) f -> p n f", p=128), VL, kVL, ["vb"])
            if STOP == 'A2':
                P.final_wait("sp", ["kb", "vb"])
                raise _Stop()
            AG("agk", kb_d, kg_d, ["kb"], ["kg"])
            if STOP == 'A3':
                raise _Stop()
            AG("agv", vb_d, vg_d, ["vb"], ["vg"])
            if l > 0:
                DMA("sp", "XG", XG, xg_d.rearrange("(r c p) t -> p r c t", r=4, p=128), ["xg"], [tk(0)])
                for side, (mcol, src) in enumerate(((0, slice(HALO, 2 * HALO)), (4, slice(0, HALO)))):
                    dst = XH[:, :, side * HALO:(side + 1) * HALO]
                    TS("dve", dst, XG[:, 0, :, src], CST[:, mcol:mcol + 1], None, ALU.mult, ALU.bypass, [tk(0), "CST"], ["XH"])
                    for r in range(1, 4):
                        STT(dst, XG[:, r, :, src], CST[:, mcol + r:mcol + r + 1], dst, ALU.mult, ALU.add, [tk(0), "CST", "XH"], ["XH"])

            for c in range(8):
                t = T[c % 2]
                ACT(t[:, 0:2 * HALO], XH[:, c, :], AF.Square, ["XH"], [tk(c % 2)])
                MM(PS[5][:, 0:2 * HALO], ONF[:], t[:, 0:2 * HALO], c == 0, c == 7, ["ONF", tk(c % 2)], [("ps", 5)])
            ACT(RSH[:], PS[5][:, 0:2 * HALO], AF.Ln, [("ps", 5)], ["RSH"], scale=1.0 / 1024, bias=EPS)
            ACT(RSH[:], RSH[:], AF.Exp, ["RSH"], ["RSH"], scale=-0.5)
            for c in range(8):
                STT(HX[:, c, 0:HALO], XH[:, c, 0:HALO], VEC[:, c:c + 1], RSH[:, 0:HALO], ALU.mult, ALU.mult, ["XH", "VEC", "RSH"], ["HX", "HXa"])
                STT(HX[:, c, NT + HALO:NT + 2 * HALO], XH[:, c, HALO:2 * HALO], VEC[:, c:c + 1], RSH[:, HALO:2 * HALO], ALU.mult, ALU.mult, ["XH", "VEC", "RSH"], ["HX", "HXa"])

            if STOP == 'A':
                raise _Stop()
            DMA("pool", "WB", Wl[:, 0:3, :], w_in_d[l, 0:384, 1024:2560].rearrange("(c p) e -> p c e", p=128), [], ["WB", "WBk"])
            ptkeys = lambda i: [tk(i)] + ([("PT", q) for q in range(NPT) if PTT[q // 2] == i])

            def lru_unit(tb, j):
                def gen(k):
                    s = HALO + tb * TB
                    blk = slice(tb * TB, (tb + 1) * TB)
                    tl = [T[7 * k + i] for i in range(7)]
                    kl = [ptkeys(7 * k + i) for i in range(7)]
                    UE, kUE = tl[0], kl[0]
                    XC, kXC = tl[1][:, 0:TB], kl[1]
                    t = lambda i: tl[2 + i][:, 0:TB]
                    kt_ = lambda i: kl[2 + i]
                    t4, k4 = tl[0][:, 0:TB], kl[0]
                    pA, pC = 2 * k, 2 * k + 1
                    cj = slice(j * 128, (j + 1) * 128)
                    zb = ZC[:, 0:1].to_broadcast([128, TB])
                    for c in range(8):
                        MM(PS[pA][:], Wl[:, c, cj], HX[:, c, s - 1:s + 511], c == 0, c == 7, ["WB", "HX"], [("ps", pA)])
                    for c in range(8):
                        MM(PS[pC][:, 0:3], Wl[:, c, cj], HX[:, c, s + 511:s + 514], c == 0, c == 7, ["WB", "HX"], [("ps", pC)])
                    yield
                    ACT(UE[:, 0:512], PS[pA][:], AF.Copy, [("ps", pA)], kUE)
                    ACT(UE[:, 512:515], PS[pC][:, 0:3], AF.Copy, [("ps", pC)], kUE)
                    for c in range(8):
                        MM(PS[pA][:], Wl[:, c, 768 + j * 128:768 + (j + 1) * 128], HX[:, c, s:s + TB], c == 0, c == 7, ["WB", "HX"], [("ps", pA)])
                    yield
                    cw = lambda q: VEC[:, 12 + q * 6 + j:13 + q * 6 + j]
                    TS("dve", XC, UE[:, 0:512], cw(0), VEC[:, 36 + j:37 + j], ALU.mult, ALU.add, [kUE[0], "VEC"], kXC)
                    STT(XC, UE[:, 1:513], cw(1), XC, ALU.mult, ALU.add, [kUE[0], "VEC", kXC[0]], [kXC[0]])
                    yield
                    STT(XC, UE[:, 2:514], cw(2), XC, ALU.mult, ALU.add, [kUE[0], "VEC", kXC[0]], [kXC[0]])
                    STT(XC, UE[:, 3:515], cw(3), XC, ALU.mult, ALU.add, [kUE[0], "VEC", kXC[0]], [kXC[0]])
                    ACT(XCB[k][:], XC, AF.Copy, [kXC[0]], [("XCB", k)])
                    yield
                    sigmoid_from(t4, k4[0], PS[pA][:], ("ps", pA), scale=-1.0)
                    TT("dve", t4, t4, PS[pA][:], ALU.mult, [k4[0], ("ps", pA)], [k4[0]])
                    yield
                    for d in range(2):
                        MM(PS[pC][:], WR[:, d, j, :], XCB[k][:], True, True, kWRI + [("XCB", k)], [("ps", pC)])
                        sigmoid_from(t(0), kt_(0)[0], PS[pC][:], ("ps", pC), scale=-1.0, bias=NB[:, d * 6 + j:d * 6 + j + 1])
                        MM(PS[pC][:], WI[:, d, j, :], XCB[k][:], True, True, kWRI + [("XCB", k)], [("ps", pC)])
                        yield
                        ta2 = t(2) if d == 0 else t(4)
                        ka2 = kt_(2) if d == 0 else kt_(4)
                        ACT(t(1), t(0), AF.Exp, [kt_(0)[0], "LSL"], kt_(1), scale=LSL[:, d * 6 + j:d * 6 + j + 1])
                        ACT(ta2, t(0), AF.Exp, [kt_(0)[0], "LSL2"], ka2, scale=LSL2[:, d * 6 + j:d * 6 + j + 1])
                        TS("dve", ta2, ta2, -1.0, 1.0, ALU.mult, ALU.add, [ka2[0]], [ka2[0]])
                        yield
                        sigmoid_from(t(3), kt_(3)[0], PS[pC][:], ("ps", pC), scale=-1.0, bias=NB[:, 12 + d * 6 + j:12 + d * 6 + j + 1])
                        ACT(ta2, ta2, AF.Ln, [ka2[0]], [ka2[0]], bias=1e-30)
                        ACT(ta2, ta2, AF.Exp, [ka2[0]], [ka2[0]], scale=0.5)
                        yield
                        TT("dve", t(3), t(3), XC, ALU.mult, [kt_(3)[0], kXC[0]], [kt_(3)[0]])
                        TT("dve", t(3), t(3), ta2, ALU.mult, [kt_(3)[0], ka2[0]], [kt_(3)[0]])
                        yield
                        if d == 0:
                            first = tb == 0
                            SCAN(t(2), t(1), t(3), 0.0 if first else HS[:, j:j + 1], [kt_(1)[0], kt_(3)[0], "HS"], [kt_(2)[0]])
                            SCAN(t(0), t(1), zb, 1.0 if first else HS[:, 6 + j:7 + j], [kt_(1)[0], "ZC", "HS"], [kt_(0)[0]])
                            yield
                            ACT(HS[:, j:j + 1], tl[4][:, TB - 1:TB], AF.Copy, [kt_(2)[0]], ["HS"])
                            ACT(HS[:, 6 + j:7 + j], tl[2][:, TB - 1:TB], AF.Copy, [kt_(0)[0]], ["HS"])
                            TT("dve", CF[:, j, blk], t(0), t4, ALU.mult, [kt_(0)[0], k4[0]], kCF)
                            yield
                        else:
                            SCAN(t(4)[:, ::-1], t(1)[:, ::-1], t(3)[:, ::-1], 0.0, [kt_(1)[0], kt_(3)[0]], [kt_(4)[0]])
                            SCAN(t(0)[:, ::-1], t(1)[:, ::-1], zb, 1.0, [kt_(1)[0], "ZC"], [kt_(0)[0]])
                            yield
                            ACT(EB[:, tb, j:j + 1], tl[6][:, 0:1], AF.Copy, [kt_(4)[0]], ["EB"])
                            ACT(PB[:, tb, j:j + 1], tl[2][:, 0:1], AF.Copy, [kt_(0)[0]], ["PB"])
                            TT("dve", t(2), t(2), t(4), ALU.add, [kt_(2)[0], kt_(4)[0]], [kt_(2)[0]])
                            TT("dve", YB[:, j, :], t(2), t4, ALU.mult, [kt_(2)[0], k4[0]], [("YB", j)])
                            TT("dve", XCB[k][:], t(0), t4, ALU.mult, [kt_(0)[0], k4[0]], [("XCB", k)])
                            DMA("sp", "cbs", cb_d[:, j, blk], XCB[k][:], [("XCB", k)], [("cbd", tb)])
                            if j == 5:
                                outproj_add(WoL, "WB", 6, [("YB", jj) for jj in range(6)], tb, banks=(4, 5))
                            yield
                return gen

            run_interleaved([lru_unit(tb, j) for tb in range(NTB) for j in range(6)], 2)
            P.op("pool", lambda e: e.memset(HS[:, 12:18], 0.0), [], ["HSb"])
            P.op("pool", lambda e: e.memset(HS[:, 18:24], 1.0), [], ["HSb"])
            for tb in range(NTB - 1, -1, -1):
                TT("dve", HS[:, 12:18], HS[:, 12:18], PB[:, tb, :], ALU.mult, ["HSb", "PB"], ["HSb"])
                TT("dve", HS[:, 12:18], HS[:, 12:18], EB[:, tb, :], ALU.add, ["HSb", "EB"], ["HSb"])
                TT("dve", HS[:, 18:24], HS[:, 18:24], PB[:, tb, :], ALU.mult, ["HSb", "PB"], ["HSb"])
            DMA("sp", "sbn", sb_d, HS[:], ["HS", "HSb"], ["sbn"])
            AG("ags", sb_d, sg_d, ["sbn"], ["sgt"])
            Wp = wview(0, 8, 1024)
            PW = BIG[:, 15360:15872].rearrange("p (g e) -> p g e", g=4)
            WoP = WB[:, 8192:12288].rearrange("p (j m) -> p j m", j=4)
            DMA("pool", "WB", Wp, w_in_d[l, :, 0:1024].rearrange("(c p) e -> p c e", p=128), [], ["WB"])
            DMA("pool", "WRI", PW, pool_w_d[l].rearrange("g c e -> c g e"), [], kWRI)
            DMA("pool", "WB", WoP, w_out_d[l, 0:512, :].rearrange("(j p) m -> p j m", p=128), [], ["WB"])

            def pool_unit(tb, g):
                def gen(k):
                    w = POOL_W[g]
                    s = HALO + tb * TB
                    half = w // 2
                    cg = slice(g * 128, (g + 1) * 128)
                    UE, kUE = T[6 * k], tk(6 * k)
                    pA, pC = 2 * k, 2 * k + 1
                    for c in range(8):
                        MM(PS[pA][:], Wp[:, c, cg], HX[:, c, s - half:s - half + 512], c == 0, c == 7, ["WB", "HX"], [("ps", pA)])
                    for c in range(8):
                        MM(PS[pC][:, 0:w - 1], Wp[:, c, cg], HX[:, c, s - half + 512:s + 511 + half], c == 0, c == 7, ["WB", "HX"], [("ps", pC)])
                    yield
                    ACT(UE[:, 0:512], PS[pA][:], AF.Copy, [("ps", pA)], [kUE])
                    ACT(UE[:, 512:511 + w], PS[pC][:, 0:w - 1], AF.Copy, [("ps", pC)], [kUE])
                    for c in range(8):
                        MM(PS[pA][:], Wp[:, c, 512 + g * 128:512 + (g + 1) * 128], HX[:, c, s:s + TB], c == 0, c == 7, ["WB", "HX"], [("ps", pA)])
                    yield
                    L = 511 + w
                    src, sk = UE, kUE
                    step = 1
                    i = 0
                    while step < w:
                        L -= step
                        dst, dk = T[6 * k + 1 + i % 2], tk(6 * k + 1 + i % 2)
                        wk_ = [dk] + ([("PT", q) for q in range(NPT)] if 6 * k + 1 + i % 2 == 7 else [])
                        TT("dve", dst[:, 0:L], src[:, 0:L], src[:, step:step + L], ALU.add, [sk], wk_)
                        src, sk = dst, dk
                        step *= 2
                        i += 1
                        yield
                    if tb == 0:
                        TT("dve", src[:, 0:8], src[:, 0:8], CST[:, 16 + g * 16:24 + g * 16], ALU.mult, [sk, "CST"], [sk])
                    if tb == NTB - 1:
                        TT("dve", src[:, 504:512], src[:, 504:512], CST[:, 24 + g * 16:32 + g * 16], ALU.mult, [sk, "CST"], [sk])
                    STT(XCB[k][:], src[:, 0:512], 1.0 / w, UE[:, half:half + 512], ALU.mult, ALU.subtract, [sk, kUE], [("XCB", k)])
                    t3, k3 = T[6 * k + 3][:, 0:TB], tk(6 * k + 3)
                    sigmoid_from(t3, k3, PS[pA][:], ("ps", pA), scale=-1.0)
                    yield
                    MM(PS[pC][:], PW[:, g, :], XCB[k][:], True, True, kWRI + [("XCB", k)], [("ps", pC)])
                    TT("dve", t3, t3, PS[pA][:], ALU.mult, [k3, ("ps", pA)], [k3])
                    yield
                    STT(YB[:, g, :], PS[pC][:], VEC[:, 8 + g:9 + g], t3, ALU.mult, ALU.mult, [("ps", pC), "VEC", k3], [("YB", g)])
                    if g == 3:
                        outproj_add(WoP, "WB", 4, [("YB", jj) for jj in range(4)], tb, banks=(4, 5))
                    yield
                return gen

            run_interleaved([pool_unit(tb, g) for tb in range(NTB) for g in range(4)], 2)

            if STOP == 'pool':
                raise _Stop()
            DMA("sp", "SG4", SG4[:], sg_d.rearrange("(r p) f -> p r f", p=128), ["sgt"], ["SG4"])
            for d in range(2):
                order = (0, 1, 2) if d == 0 else (3, 2, 1)
                co = 12 * d
                dst = CARRY if d == 0 else CARB
                dk = "CARRY" if d == 0 else "CARB"
                P.op("pool", lambda e: e.memset(CAR[0][:], 0.0), [], ["CAR0"])
                P.op("pool", lambda e, dst=dst: e.memset(dst[:], 0.0), [], [dk])
                for i, kk in enumerate(order):
                    TT("dve", CAR[i + 1][:], SG4[:, kk, co + 6:co + 12], CAR[i][:], ALU.mult, ["SG4", "CAR%d" % i], ["CAR%d" % (i + 1)])
                    TT("dve", CAR[i + 1][:], CAR[i + 1][:], SG4[:, kk, co:co + 6], ALU.add, ["SG4", "CAR%d" % (i + 1)], ["CAR%d" % (i + 1)])
                    knext = kk + 1 if d == 0 else kk - 1
                    STT(dst[:], CAR[i + 1][:], CST[:, 8 + knext:9 + knext], dst[:], ALU.mult, ALU.add, ["CAR%d" % (i + 1), "CST", dk], [dk])
            P.op("dve", lambda e: e.tensor_copy(out=HIN[:, NTB - 1, :], in_=CARB[:]), ["CARB"], ["HIN"])
            for tb in range(NTB - 2, -1, -1):
                TT("dve", HIN[:, tb, :], HIN[:, tb + 1, :], PB[:, tb + 1, :], ALU.mult, ["HIN", "PB"], ["HIN"])
                TT("dve", HIN[:, tb, :], HIN[:, tb, :], EB[:, tb + 1, :], ALU.add, ["HIN", "EB"], ["HIN"])
            for tb in range(NTB):
                blk = slice(tb * TB, (tb + 1) * TB)
                cbs = []
                for q in range(3):
                    v = T[q][:, 0:TB].bitcast(BF16)
                    cbs += [v[:, 0:TB], v[:, TB:2 * TB]]
                for q in range(3):
                    DMA("sp", "CBS%d" % q, T[q][:, 0:TB].bitcast(BF16).rearrange("p (j t) -> p j t", j=2), cb_d[:, 2 * q:2 * q + 2, blk], [("cbd", tb)], [tk(q)])
                for j in range(6):
                    TS("dve", YB[:, j, :], CF[:, j, blk], CARRY[:, j:j + 1], None, ALU.mult, ALU.bypass, kCF + ["CARRY"], [("YB", j)])
                    STT(YB[:, j, :], cbs[j], HIN[:, tb, j:j + 1], YB[:, j, :], ALU.mult, ALU.add, [tk(j // 2), "HIN", ("YB", j)], [("YB", j)])
                outproj_add(WoL, "WB", 6, [("YB", jj) for jj in range(6)], tb, banks=(4, 5))

            DMA("sp", "KT", KT, kg_d.rearrange("(r h p) t -> p r h t", h=2, p=128)[:, :, 0, :], ["kg"], kKT)
            DMA("sp", "VA", VA, vg_d.rearrange("(n p) f -> p n f", p=128)[:, :, 0:128], ["vg"], kVA)
            scl = 128.0 ** -0.5
            ACC = (T[8][:, 0:TB], T[9][:, 0:TB])
            for kvh in range(2):
                if kvh == 1:
                    DMA("sp", "KT", KT, kg_d.rearrange("(r h p) t -> p r h t", h=2, p=128)[:, :, kvh, :], ["kg"], kKT)
                    DMA("sp", "VA", VA, vg_d.rearrange("(n p) f -> p n f", p=128)[:, :, kvh * 128:(kvh + 1) * 128], ["vg"], kVA)
                wo = kvh * 9216
                wkey = "WBa%d" % kvh
                Wq = wview(wo, 8, 384)
                Wg = wview(wo + 3072, 8, 384)
                WoA = WB[:, wo + 6144:wo + 9216].rearrange("p (j m) -> p j m", j=3)

                def load_attn_w(kv):
                    o2 = kv * 9216
                    kk = ["WBa%d" % kv, "WB"]
                    DMA("pool", "WBa%d" % kv, wview(o2, 8, 384), w_in_d[l, :, 2560 + kv * 384:2560 + (kv + 1) * 384].rearrange("(c p) e -> p c e", p=128), [], kk)
                    DMA("pool", "WBa%d" % kv, wview(o2 + 3072, 8, 384), w_in_d[l, :, 3840 + kv * 384:3840 + (kv + 1) * 384].rearrange("(c p) e -> p c e", p=128), [], kk)
                    DMA("pool", "WBa%d" % kv, WB[:, o2 + 6144:o2 + 9216].rearrange("p (j m) -> p j m", j=3), w_out_d[l, 1280 + kv * 384:1280 + (kv + 1) * 384, :].rearrange("(j p) m -> p j m", p=128), [], kk)

                if kvh == 0:
                    load_attn_w(0)
                    load_attn_w(1)
                cs_begin([tb * 4 + n4 for tb in range(NTB) for n4 in range(4)])
                T6v = T[6][:, 0:TB].bitcast(BF16)
                T9v = T[9][:, 0:TB].bitcast(BF16)
                T12v = T[12][:, 0:TB].bitcast(BF16)
                T13v = T[13][:, 0:TB].bitcast(BF16)
                QTB = [[T12v[:, 0:TB], T12v[:, TB:2 * TB], T13v[:, 0:TB]], [T6v[:, 0:TB], T6v[:, TB:2 * TB], T9v[:, 0:TB]]]
                qtw = [[("QT", 0), tk(12), tk(13)], [("QT", 1), tk(6), tk(9)]]

                def qtile(tb, n4, buf):
                    def gen(k):
                        s = HALO + tb * TB
                        pb = 3 + k
                        for c in range(8):
                            MM(PS[pb][:, 0:384], HX[:, c, s + n4 * 128:s + (n4 + 1) * 128], Wq[:, c, :], c == 0, c == 7, ["HX", wkey], [("ps", pb)])
                        yield from normrope_gen(PS[pb][:, 0:384], ("ps", pb), 3, QKG[:, 0:128], XCB[k][:, 0:384], ("XCB", k), k)
                        for hh in range(3):
                            TR(PST[k][:, hh * 128:(hh + 1) * 128], XCB[k][:, hh * 128:(hh + 1) * 128], [("XCB", k)], [("pst", k)])
                        yield
                        for hh in range(3):
                            ACT(QTB[buf][hh][:, n4 * 128:(n4 + 1) * 128], PST[k][:, hh * 128:(hh + 1) * 128], AF.Copy, [("pst", k)], qtw[buf])
                        yield
                    return gen

                def qprep_inloop(tb, buf):
                    for n4 in range(4):
                        yield from qtile(tb, n4, buf)(0)

                run_interleaved([qtile(0, n4, 0) for n4 in range(4)], 2)
                for tb in range(NTB):
                    s = HALO + tb * TB
                    buf = tb % 2
                    prep = qprep_inloop(tb + 1, 1 - buf) if tb + 1 < NTB else None
                    if not QPREP_INLOOP and prep is not None:
                        for _ in prep:
                            pass
                        prep = None
                    it = 0
                    for hh in range(3):
                        for c in range(8):
                            MM(PS[3][:], Wg[:, c, hh * 128:(hh + 1) * 128], HX[:, c, s:s + TB], c == 0, c == 7, [wkey, "HX"], [("ps", 3)])
                        t5, k5 = T[5][:, 0:TB], tk(5)
                        t4, k4 = T[4][:, 0:TB], tk(4)
                        sigmoid_from(t5, k5, PS[3][:], ("ps", 3), scale=-1.0)
                        TT("dve", t5, t5, PS[3][:], ALU.mult, [k5, ("ps", 3)], [k5])

                        def QK(kt):
                            b = kt % 3
                            MM(PS[b][:], KT[:, kt // 16, (kt % 16) * 128:(kt % 16 + 1) * 128], QTB[buf][hh], True, True, kKT + [("QT", buf)], [("ps", b)])
                        QK(0)
                        QK(1)
                        def consume(kt):
                            pb = kt % NPT
                            MM(PS[4][:], VA[:, kt, :], PT[pb], kt == 0, kt == 63, kVA + [("PT", pb)], [("ps", 4)])
                            if kt % 3 == 2:
                                MM(PS[5][:], ONB[:], PT[pb], kt == 2, False, ["ONB", ("PT", pb)], [("ps", 5)])
                            elif kt == 0:
                                P.op("dve", lambda e, pb=pb: e.tensor_copy(out=ACC[0], in_=PT[pb]), [("PT", pb)], [tk(8)])
                            else:
                                TT("dve", ACC[0], ACC[0], PT[pb], ALU.add, [tk(8), ("PT", pb)], [tk(8)])

                        for kt in range(64):
                            b = kt % 3
                            pb = kt % NPT
                            ACT(PT[pb], PS[b][:], AF.Exp, [("ps", b)], [("PT", pb), ptk(pb)], scale=scl)
                            if kt + 2 < 64:
                                QK(kt + 2)
                            if kt >= 1:
                                consume(kt - 1)
                            it += 1
                            if prep is not None and it % 6 == 0:
                                try:
                                    next(prep)
                                except StopIteration:
                                    prep = None
                        consume(63)
                        MM(PS[5][:], ONF[:], ACC[0], False, True, ["ONF", tk(8)], [("ps", 5)])
                        ACT(t4, PS[5][:], AF.Ln, [("ps", 5)], [k4])
                        ACT(t4, t4, AF.Exp, [k4], [k4], scale=-1.0)
                        TT("dve", t4, t4, t5, ALU.mult, [k4, k5], [k4])
                        TT("dve", YB[:, hh, :], PS[4][:], t4, ALU.mult, [("ps", 4), k4], [("YB", hh)])
                    if prep is not None:
                        for _ in prep:
                            pass
                    outproj_add(WoA, wkey, 3, [("YB", jj) for jj in range(3)], tb, banks=(3, 5))

        try:
            for l in range(nl):
                layer(l)
        except _Stop:
            pass
        DMA("sp", "yT", yT_d.rearrange("(c p) t -> p c t", p=128), X[:], [("X", m) for m in range(8)], ["yT"])
        P.final_wait("sp", ["yT", "kg", "vg", "sgt", "xg"])
        print("SBUF remaining bytes/partition:", nc.sbuf_bytes_remaining)
        P.emit()
    return nc


_NC_CACHE = {}


def _get_nc(nl):
    if nl not in _NC_CACHE:
        _NC_CACHE[nl] = build(nl)
    return _NC_CACHE[nl]


def _host_consts():
    inv = (10000.0 ** (-np.arange(0, 64, 2, dtype=np.float32) / 64.0)).astype(np.float32)
    out = []
    for c in range(NCORES):
        r = c % 4
        t = np.arange(r * NT, (r + 1) * NT)
        row = (t // 64).astype(np.float32)
        col = (t % 64).astype(np.float32)
        ang = np.concatenate([row[:, None] * inv[None, :], col[:, None] * inv[None, :]], axis=1).astype(np.float32)
        cs = np.concatenate([np.cos(ang), np.sin(ang)], axis=1).astype(np.float32)
        cst = np.zeros((80,), np.float32)
        if r > 0:
            cst[r - 1] = 1.0
        if r < 3:
            cst[4 + r + 1] = 1.0
        cst[8 + r] = 1.0
        L = 4 * NT
        for g, w in enumerate(POOL_W):
            half = w // 2
            fl = np.ones(8, np.float32)
            fr = np.ones(8, np.float32)
            if r == 0:
                tt = np.arange(0, 8)
                cnt = np.clip(tt + half, 0, L) - np.clip(tt - half, 0, L)
                fl = (w / cnt).astype(np.float32)
            if r == 3:
                tt = np.arange(L - 8, L)
                cnt = np.clip(tt + half, 0, L) - np.clip(tt - half, 0, L)
                fr = (w / cnt).astype(np.float32)
            cst[16 + g * 16:24 + g * 16] = fl
            cst[24 + g * 16:32 + g * 16] = fr
        out.append((cs, np.ascontiguousarray(np.broadcast_to(cst[None, :], (128, 80)))))
    return out


def _pack_vecs(norm_g, pool_scale, conv_w, conv_b, lru_br, lru_bi, lru_lam):
    nl = norm_g.shape[0]
    v = np.zeros((nl, 128, NV), np.float32)
    fm = lambda a, n: a.reshape(n, 128).T
    for l in range(nl):
        v[l, :, 0:8] = fm(norm_g[l], 8)
        v[l, :, 8:12] = fm(pool_scale[l], 4)
        for k in range(4):
            v[l, :, 12 + k * 6:18 + k * 6] = fm(conv_w[l, k], 6)
        v[l, :, 36:42] = fm(conv_b[l], 6)
        for d in range(2):
            v[l, :, 42 + d * 6:48 + d * 6] = fm(lru_br[l, d], 6)
            v[l, :, 54 + d * 6:60 + d * 6] = fm(lru_bi[l, d], 6)
            v[l, :, 66 + d * 6:72 + d * 6] = fm(lru_lam[l, d], 6)
    return v


LAYERS_PER_LAUNCH = 4
KT_SETS = 2
QPREP_INLOOP = True
PST_SPLIT = 512
LRU_SETS = 2
STOP = None


def kernel(x, norm_g, w_in, pool_w, pool_scale, conv_w, conv_b, lru_wr, lru_br, lru_wi, lru_bi,
           lru_lam, q_norm, k_norm, w_out):
    f = lambda a: np.ascontiguousarray(np.asarray(a, dtype=np.float32))
    x = f(x)
    vecs = _pack_vecs(f(norm_g), f(pool_scale), f(conv_w), f(conv_b), f(lru_br), f(lru_bi), f(lru_lam))
    qkn = np.ascontiguousarray(np.concatenate([f(q_norm), f(k_norm)], axis=1)[:, None, :])
    w_in, w_out, pool_w, lru_wr, lru_wi = f(w_in), f(w_out), f(pool_w), f(lru_wr), f(lru_wi)
    consts = _host_consts()
    nlp = LAYERS_PER_LAUNCH
    nc = _get_nc(nlp)
    cur = x
    for l0 in range(0, DEPTH, nlp):
        ls = slice(l0, l0 + nlp)
        in_maps = []
        for c in range(NCORES):
            b, r = c // 4, c % 4
            xs = cur[b, r * NT:(r + 1) * NT, :]
            xh = np.zeros((2 * HALO, 1024), np.float32)
            if r > 0:
                xh[0:HALO] = cur[b, r * NT - HALO:r * NT, :]
            if r < 3:
                xh[HALO:] = cur[b, (r + 1) * NT:(r + 1) * NT + HALO, :]
            in_maps.append({
                "xT": np.ascontiguousarray(xs.T), "xh": np.ascontiguousarray(xh.T),
                "cs": consts[c][0], "cst": consts[c][1],
                "vecs": np.ascontiguousarray(vecs[ls]), "qkn": np.ascontiguousarray(qkn[ls]),
                "w_in": w_in[ls], "w_out": w_out[ls], "pool_w": pool_w[ls],
                "lru_wr": lru_wr[ls], "lru_wi": lru_wi[ls],
            })
        res = run_bass_kernel_spmd(nc, in_maps, core_ids=list(range(NCORES)))
        nxt = np.empty_like(cur)
        for c in range(NCORES):
            b, r = c // 4, c % 4
            nxt[b, r * NT:(r + 1) * NT, :] = np.asarray(res.results[c]["yT"]).T
        cur = nxt
    return cur
```

```python
import contextlib
import numpy as np
import concourse.bass as bass
import concourse.mybir as mybir
from concourse.bass_utils import run_bass_kernel_spmd

F32 = mybir.dt.float32
BF16 = mybir.dt.bfloat16
AF = mybir.ActivationFunctionType
ALU = mybir.AluOpType
AX = mybir.AxisListType

ENGS = ("pe", "act", "dve", "pool", "sp")
NCORES = 8
DEPTH = 4
NT = 2048
TB = 512
NTB = NT // TB
HALO = 8
EPS = 1e-6
NV = 78
RG = [[0, 1, 2, 3], [4, 5, 6, 7]]
POOL_W = (2, 4, 8, 16)


class _Stop(Exception):
    pass


class Prog:
    def __init__(self, nc, stack):
        self.nc = nc
        self.stack = stack
        self.streams = {e: [] for e in ENGS}
        self.esem = {e: stack.enter_context(nc.semaphore("s_" + e)) for e in ENGS if e != "sp"}
        self.ecnt = {e: 0 for e in ENGS}
        self.dsem = {}
        self.dcnt = {}
        self.last_w = {}
        self.readers = {}
        self.waited = {e: {} for e in ENGS}
        self.nops = 0

    def _dma_sem(self, name):
        if name not in self.dsem:
            self.dsem[name] = self.stack.enter_context(self.nc.semaphore("d_" + name))
            self.dcnt[name] = 0
        return self.dsem[name]

    def _need(self, eng, reads, writes, rawonly=()):
        need = {}

        def add(ev):
            if ev is None:
                return
            sem, val, src = ev
            if src == "pe" and eng == "pe":
                return
            k = id(sem)
            if k not in need or need[k][1] < val:
                need[k] = (sem, val)

        for k in reads:
            add(self.last_w.get(k))
        for k in rawonly:
            add(self.last_w.get(k))
        for k in writes:
            add(self.last_w.get(k))
            for ev in self.readers.get(k, {}).values():
                add(ev)
        out = []
        for k, (sem, val) in need.items():
            if self.waited[eng].get(k, 0) < val:
                self.waited[eng][k] = val
                out.append((sem, val))
        return out

    def _record(self, ev, reads, writes):
        for k in writes:
            self.last_w[k] = ev
            self.readers[k] = {}
        for k in reads:
            d = self.readers.setdefault(k, {})
            kk = id(ev[0])
            if kk not in d or d[kk][1] < ev[1]:
                d[kk] = ev

    def op(self, eng, fn, reads=(), writes=(), rawonly=()):
        waits = self._need(eng, reads, writes, rawonly)
        self.ecnt[eng] += 1
        sem = self.esem[eng]
        self.streams[eng].append((waits, fn, sem, 1))
        self.nops += 1
        self._record((sem, self.ecnt[eng], eng), reads, writes)

    def dma(self, q, group, fn, reads=(), writes=(), inc=16):
        waits = self._need(q, reads, writes)
        sem = self._dma_sem(group)
        self.dcnt[group] += inc
        self.streams[q].append((waits, fn, sem, inc))
        self.nops += 1
        self._record((sem, self.dcnt[group], "dma"), reads, writes)

    def final_wait(self, eng, keys):
        waits = self._need(eng, keys, ())
        self.streams[eng].append((waits, None, None, 0))

    def emit(self):
        with self.nc.Block() as block:
            def replay(name):
                def body(eng):
                    for waits, fn, sem, inc in self.streams[name]:
                        for (s, v) in waits:
                            eng.wait_ge(s, v)
                        if fn is not None:
                            fn(eng).then_inc(sem, inc)
                return body
            block.tensor(replay("pe"))
            block.scalar(replay("act"))
            block.vector(replay("dve"))
            block.gpsimd(replay("pool"))
            block.sync(replay("sp"))


def rsl(lo, hi):
    return slice(hi - 1, lo - 1 if lo > 0 else None, -1)


def run_interleaved(factories, nsets, bg=None):
    free = list(range(nsets))
    active = []
    it = iter(factories)

    def fill():
        while free:
            try:
                f = next(it)
            except StopIteration:
                return
            k = free.pop(0)
            active.append((f(k), k))

    fill()
    while active:
        for g, k in list(active):
            try:
                next(g)
            except StopIteration:
                active.remove((g, k))
                free.append(k)
                fill()
        if bg is not None:
            try:
                next(bg)
            except StopIteration:
                bg = None
    if bg is not None:
        for _ in bg:
            pass


def build(nl):
    nc = bass.Bass("TRN2", target_bir_lowering=False)
    dI = lambda name, shape, dt=F32: nc.dram_tensor(name, shape, dt, kind="ExternalInput").ap()
    xT_d = dI("xT", [1024, NT])
    xh_d = dI("xh", [1024, 2 * HALO])
    cs_d = dI("cs", [NT, 128])
    cst_d = dI("cst", [128, 80])
    vecs_d = dI("vecs", [nl, 128, NV])
    qkn_d = dI("qkn", [nl, 1, 256])
    w_in_d = dI("w_in", [nl, 1024, 4608])
    w_out_d = dI("w_out", [nl, 2048, 1024])
    pool_w_d = dI("pool_w", [nl, 4, 128, 128])
    wr_d = dI("lru_wr", [nl, 2, 6, 128, 128])
    wi_d = dI("lru_wi", [nl, 2, 6, 128, 128])
    yT_d = nc.dram_tensor("yT", [1024, NT], F32, kind="ExternalOutput").ap()
    kb_d = nc.dram_tensor("kb", [256, NT], BF16).ap()
    kg_d = nc.dram_tensor("kg", [4 * 256, NT], BF16).ap()
    vb_d = nc.dram_tensor("vb", [NT, 256], BF16).ap()
    vg_d = nc.dram_tensor("vg", [4 * NT, 256], BF16).ap()
    sb_d = nc.dram_tensor("sbn", [128, 24], F32).ap()
    sg_d = nc.dram_tensor("sgt", [4 * 128, 24], F32).ap()
    cb_d = nc.dram_tensor("cbs", [128, 6, NT], BF16).ap()
    xe_d = nc.dram_tensor("xe", [1024, 2 * HALO], F32).ap()
    xg_d = nc.dram_tensor("xg", [4 * 1024, 2 * HALO], F32).ap()

    with contextlib.ExitStack() as st:
        P = Prog(nc, st)
        sbt = lambda name, shape, dt=F32: st.enter_context(nc.sbuf_tensor(name, shape, dt))
        X = sbt("X", [128, 8, NT])
        HX = sbt("HX", [128, 8, NT + 2 * HALO], BF16)
        BIG = sbt("BIG", [128, 16384], BF16)
        WB = sbt("WB", [128, 18432], BF16)
        XH = sbt("XH", [128, 8, 2 * HALO])
        CSB = [sbt("CSB%d" % i, [128, 128]) for i in range(2)]
        CST = sbt("CST", [128, 80])
        VEC = sbt("VEC", [128, NV])
        QKG = sbt("QKG", [128, 256])
        NB = sbt("NB", [128, 24])
        LSL = sbt("LSL", [128, 12])
        LSL2 = sbt("LSL2", [128, 12])
        LT = [sbt("LT%d" % i, [128, 12]) for i in range(4)]
        IDB = sbt("IDB", [128, 128], BF16)
        ONF = sbt("ONF", [128, 128])
        ONB = sbt("ONB", [128, 128], BF16)
        ZC = sbt("ZC", [128, 1])
        T = [sbt("T%d" % i, [128, 528]) for i in range(14)]
        XCB = [sbt("XCB%d" % i, [128, TB], BF16) for i in range(2)]
        RSH = sbt("RSH", [128, 2 * HALO])
        YB = sbt("YB", [128, 6, TB], BF16)
        SS = [sbt("SS%d" % i, [128, 4]) for i in range(2)]
        RS = [sbt("RS%d" % i, [128, 4]) for i in range(2)]
        HS = sbt("HS", [128, 24])
        SG4 = sbt("SG4", [128, 4, 24])
        EB = sbt("EB", [128, 4, 6])
        PB = sbt("PB", [128, 4, 6])
        HIN = sbt("HIN", [128, 4, 6])
        CARB = sbt("CARB", [128, 6])
        CAR = [sbt("CAR%d" % i, [128, 6]) for i in range(5)]
        CARRY = sbt("CARRY", [128, 6])
        IDF = T[1][:, 0:128]
        PT = []
        PTT = (7, 10, 11)
        for ti in PTT:
            v = T[ti][:, 0:TB].bitcast(BF16)
            PT += [v[:, 0:TB], v[:, TB:2 * TB]]
        NPT = len(PT)
        ptk = lambda pb: tk(PTT[pb // 2])
        PS = [st.enter_context(nc.psum_tensor("ps%d" % i, [128, TB], F32)) for i in range(6)]
        PST = [st.enter_context(nc.psum_tensor("pst%d" % i, [128, 1024], BF16)) for i in range(2)]

        tk = lambda i: "T%d" % i
        KLT = BIG[:, 0:4096].rearrange("p (h t) -> p h t", h=2)
        VL = BIG[:, 4096:8192].rearrange("p (n f) -> p n f", n=16)
        CF = BIG[:, 0:12288].rearrange("p (j t) -> p j t", j=6)
        KT = BIG[:, 0:8192].rearrange("p (r t) -> p r t", r=4)
        VA = BIG[:, 8192:16384].rearrange("p (n f) -> p n f", n=64)
        kKLT, kVL, kCF, kKT, kVA, kWRI = ["bA"], ["bB"], ["bA", "bB", "bC"], ["bA", "bB"], ["bC", "bD"], ["bD"]
        XG = T[0][:, 0:512].rearrange("p (r c t) -> p r c t", r=4, c=8)

        def wview(off, c, n):
            return WB[:, off:off + c * n].rearrange("p (c n) -> p c n", c=c)

        def TT(eng, out, in0, in1, op, r, w):
            P.op(eng, lambda e: e.tensor_tensor(out=out, in0=in0, in1=in1, op=op), r, w)

        def TS(eng, out, in0, s1, s2, op0, op1, r, w):
            P.op(eng, lambda e: e.tensor_scalar(out=out, in0=in0, scalar1=s1, scalar2=s2, op0=op0, op1=op1), r, w)

        def STT(out, in0, scalar, in1, op0, op1, r, w):
            P.op("dve", lambda e: e.scalar_tensor_tensor(out=out, in0=in0, scalar=scalar, in1=in1, op0=op0, op1=op1), r, w)

        def ACT(out, in_, func, r, w, **kw):
            P.op("act", lambda e: e.activation(out=out, in_=in_, func=func, **kw), r, w)

        def MM(out, lhsT, rhs, start, stop, r, w, rawonly=()):
            P.op("pe", lambda e: e.matmul(out, lhsT=lhsT, rhs=rhs, start=start, stop=stop), r, w, rawonly)

        def TR(out, in_, r, w):
            P.op("pe", lambda e: e.transpose(out=out, in_=in_, identity=IDB[:]), list(r) + ["IDB"], w)

        def RECIP(out, in_, r, w):
            P.op("dve", lambda e: e.reciprocal(out=out, in_=in_), r, w)

        def SCAN(out, d0, d1, init, r, w):
            P.op("dve", lambda e: e.tensor_tensor_scan(out=out, data0=d0, data1=d1, initial=init, op0=ALU.mult, op1=ALU.add), r, w)

        def DMA(q, group, out, in_, r, w):
            P.dma(q, group, lambda e: e.dma_start(out=out, in_=in_), r, w)

        def AG(group, in_, out, r, w):
            P.dma("pool", group, lambda e: e.collective_compute("AllGather", ALU.bypass, replica_groups=RG, ins=[in_.opt()], outs=[out.opt()]), r, w, inc=1)

        def sigmoid_from(out, ok, src, sk, **kw):
            ACT(out, src, AF.Exp, [sk], [ok], **kw)
            ACT(out, out, AF.Ln, [ok], [ok], bias=1.0)
            ACT(out, out, AF.Exp, [ok], [ok], scale=-1.0)

        def outproj_add(wo, wk, nj, ykeys, tb, rev=False, banks=(4, 5)):
            xsl = slice(tb * TB, (tb + 1) * TB)
            for m in range(8):
                b = banks[m % len(banks)]
                for j in range(nj):
                    MM(PS[b][:], wo[:, j, m * 128:(m + 1) * 128], YB[:, j, :], j == 0, j == nj - 1, [wk] + ykeys, [("ps", b)])
                src = PS[b][:, ::-1] if rev else PS[b][:]
                TT("dve", X[:, m, xsl], X[:, m, xsl], src, ALU.add, [("ps", b), ("X", m)], [("X", m)])

        cs_state = {"seq": [], "i": 0}

        def cs_begin(seq):
            cs_state["seq"] = list(seq)
            cs_state["i"] = 0
            for i in range(min(2, len(seq))):
                cs_load(i)

        def cs_load(i):
            n = cs_state["seq"][i]
            DMA("sp", "CSB%d" % (i % 2), CSB[i % 2][:], cs_d[n * 128:(n + 1) * 128, :], [], [("CSB", i % 2)])

        def cs_use():
            i = cs_state["i"]
            cs_state["i"] += 1
            return i

        def cs_done(i):
            if i + 2 < len(cs_state["seq"]):
                cs_load(i + 2)

        def normrope_gen(src, srck, H, gain, dstb, dstk, k):
            n = H * 128
            t0, t1, t2, qn = T[6 * k], T[6 * k + 1], T[6 * k + 2], T[6 * k + 3]
            k0, k1, k2, kq = tk(6 * k), tk(6 * k + 1), tk(6 * k + 2), tk(6 * k + 3)
            ci = cs_use()
            csb, csk = CSB[ci % 2], ("CSB", ci % 2)
            ACT(t0[:, 0:n], src, AF.Square, [srck], [k0])
            ACT(qn[:, 0:n], src, AF.Copy, [srck], [kq])
            yield
            P.op("dve", lambda e: e.tensor_reduce(out=SS[k][:, 0:H], in_=t0[:, 0:n].rearrange("p (h d) -> p h d", h=H), axis=AX.X, op=ALU.add), [k0], [("SS", k)])
            ACT(SS[k][:, 0:H], SS[k][:, 0:H], AF.Ln, [("SS", k)], [("SS", k)], scale=1.0 / 128, bias=EPS)
            ACT(RS[k][:, 0:H], SS[k][:, 0:H], AF.Exp, [("SS", k)], [("RS", k)], scale=-0.5)
            yield
            for h in range(H):
                STT(qn[:, h * 128:(h + 1) * 128], qn[:, h * 128:(h + 1) * 128], RS[k][:, h:h + 1], gain, ALU.mult, ALU.mult, [kq, ("RS", k), "QKG"], [kq])
            yield
            q5 = qn[:, 0:n].rearrange("p (h a b e) -> p h a b e", h=H, a=2, b=2)
            d5 = dstb.rearrange("p (h a b e) -> p h a b e", h=H, a=2, b=2)
            x1, x2 = q5[:, :, :, 0, :], q5[:, :, :, 1, :]
            cosb = csb[:, 0:64].rearrange("p (a e) -> p a e", a=2).unsqueeze(1).to_broadcast([128, H, 2, 32])
            sinb = csb[:, 64:128].rearrange("p (a e) -> p a e", a=2).unsqueeze(1).to_broadcast([128, H, 2, 32])
            hv = lambda t: t[:, 0:H * 64].rearrange("p (h a e) -> p h a e", h=H, a=2)
            TT("dve", hv(t1), x1, cosb, ALU.mult, [kq, csk], [k1])
            TT("dve", hv(t2), x2, sinb, ALU.mult, [kq, csk], [k2])
            TT("dve", d5[:, :, :, 0, :], hv(t1), hv(t2), ALU.subtract, [k1, k2], [dstk])
            yield
            TT("dve", hv(t1), x2, cosb, ALU.mult, [kq, csk], [k1])
            TT("dve", hv(t2), x1, sinb, ALU.mult, [kq, csk], [k2])
            TT("dve", d5[:, :, :, 1, :], hv(t1), hv(t2), ALU.add, [k1, k2], [dstk])
            cs_done(ci)
            yield

        P.op("pool", lambda e: e.memset(IDF, 0.0), [], [tk(1)])
        P.op("pool", lambda e: e.affine_select(out=IDF, in_=IDF, pattern=[[-1, 128]], compare_op=ALU.not_equal, fill=1.0, base=0, channel_multiplier=1), [tk(1)], [tk(1)])
        P.op("pool", lambda e: e.tensor_copy(out=IDB[:], in_=IDF), [tk(1)], ["IDB"])
        P.op("pool", lambda e: e.memset(ONF[:], 1.0), [], ["ONF"])
        P.op("pool", lambda e: e.memset(ONB[:], 1.0), [], ["ONB"])
        P.op("pool", lambda e: e.memset(ZC[:], 0.0), [], ["ZC"])
        DMA("sp", "X", X[:], xT_d.rearrange("(c p) t -> p c t", p=128), [], [("X", m) for m in range(8)])
        DMA("sp", "XH", XH[:], xh_d.rearrange("(c p) t -> p c t", p=128), [], ["XH"])
        DMA("sp", "CST", CST[:], cst_d, [], ["CST"])

        def layer(l):
            if l > 0:
                xk = [("X", m) for m in range(8)]
                DMA("sp", "xe", xe_d.rearrange("(c p) t -> p c t", p=128)[:, :, 0:HALO], X[:, :, 0:HALO], xk, ["xe"])
                DMA("sp", "xe", xe_d.rearrange("(c p) t -> p c t", p=128)[:, :, HALO:2 * HALO], X[:, :, NT - HALO:NT], xk, ["xe"])
                AG("agx", xe_d, xg_d, ["xe"], ["xg"])
            DMA("sp", "VEC", VEC[:], vecs_d[l], [], ["VEC"])
            DMA("sp", "QKG", QKG[:], qkn_d[l].partition_broadcast(128), [], ["QKG"])
            TS("dve", NB[:], VEC[:, 42:66], -1.0, None, ALU.mult, ALU.bypass, ["VEC"], ["NB"])
            lam = VEC[:, 66:78]
            TS("dve", LT[1][:], lam, -1.0, None, ALU.mult, ALU.bypass, ["VEC"], ["LT1"])
            TT("dve", LT[0][:], LT[1][:], lam, ALU.max, ["LT1", "VEC"], ["LT0"])
            ACT(LT[0][:], LT[0][:], AF.Exp, ["LT0"], ["LT0"], scale=-1.0)
            TS("dve", LT[1][:], LT[0][:], 2.0, None, ALU.add, ALU.bypass, ["LT0"], ["LT1"])
            RECIP(LT[1][:], LT[1][:], ["LT1"], ["LT1"])
            TT("dve", LT[1][:], LT[1][:], LT[0][:], ALU.mult, ["LT1", "LT0"], ["LT1"])
            TT("dve", LT[2][:], LT[1][:], LT[1][:], ALU.mult, ["LT1"], ["LT2"])
            TS("dve", LT[3][:], LT[2][:], 1.0 / 13, 1.0 / 11, ALU.mult, ALU.add, ["LT2"], ["LT3"])
            for cf in (1.0 / 9, 1.0 / 7, 1.0 / 5, 1.0 / 3, 1.0):
                TT("dve", LT[3][:], LT[3][:], LT[2][:], ALU.mult, ["LT3", "LT2"], ["LT3"])
                TS("dve", LT[3][:], LT[3][:], cf, None, ALU.add, ALU.bypass, ["LT3"], ["LT3"])
            TT("dve", LT[3][:], LT[3][:], LT[1][:], ALU.mult, ["LT3", "LT1"], ["LT3"])
            TS("dve", LT[0][:], lam, 0.0, 8.0, ALU.min, ALU.mult, ["VEC", "LT0"], ["LT0"])
            STT(LSL[:], LT[3][:], -16.0, LT[0][:], ALU.mult, ALU.add, ["LT3", "LT0"], ["LSL"])
            TS("dve", LSL2[:], LSL[:], 2.0, None, ALU.mult, ALU.bypass, ["LSL"], ["LSL2"])

            def rmsnorm_gen(tb):
                s = tb * TB
                rstd, rk = T[12 + tb % 2][:, 0:TB], tk(12 + tb % 2)
                psb = 4 + tb % 2
                for c in range(8):
                    t = T[4 + c % 2]
                    ACT(t[:, 0:TB], X[:, c, s:s + TB], AF.Square, [("X", c)], [tk(4 + c % 2)])
                    MM(PS[psb][:], ONF[:], t[:, 0:TB], c == 0, c == 7, ["ONF", tk(4 + c % 2)], [("ps", psb)])
                    if c % 2 == 1:
                        yield
                ACT(rstd, PS[psb][:], AF.Ln, [("ps", psb)], [rk], scale=1.0 / 1024, bias=EPS)
                ACT(rstd, rstd, AF.Exp, [rk], [rk], scale=-0.5)
                yield
                for c in range(8):
                    STT(HX[:, c, HALO + s:HALO + s + TB], X[:, c, s:s + TB], VEC[:, c:c + 1], rstd, ALU.mult, ALU.mult,
                        [("X", c), "VEC", rk], ["HX", "HXa"] if tb == 0 else ["HX"])
                    if c % 2 == 1:
                        yield

            for _ in rmsnorm_gen(0):
                pass

            if STOP == 'norm':
                raise _Stop()
            Wkv = wview(0, 8, 512)
            DMA("pool", "WBk", Wkv, w_in_d[l, :, 3328:3840].rearrange("(c p) e -> p c e", p=128), [], ["WBk", "WB", "WBa0", "WBa1"])
            Wl = wview(0, 8, 1536)
            WR = BIG[:, 12288:13824].rearrange("p (d j e) -> p d j e", d=2, j=6)
            WI = BIG[:, 13824:15360].rearrange("p (d j e) -> p d j e", d=2, j=6)
            WoL = WB[:, 12288:18432].rearrange("p (j m) -> p j m", j=6)
            DMA("pool", "WB", Wl[:, 3:8, :], w_in_d[l, 384:1024, 1024:2560].rearrange("(c p) e -> p c e", p=128), [], ["WB", "WBa0", "WBa1"])
            for d in range(2):
                DMA("pool", "WRI", WR[:, d], wr_d[l, d].rearrange("j c e -> c j e"), [], kWRI)
                DMA("pool", "WRI", WI[:, d], wi_d[l, d].rearrange("j c e -> c j e"), [], kWRI)
            DMA("pool", "WB", WoL, w_out_d[l, 512:1280, :].rearrange("(j p) m -> p j m", p=128), [], ["WB"])
            cs_begin(list(range(16)))

            def ktile(n):
                def gen(k):
                    pb = k
                    for c in range(8):
                        MM(PS[pb][:], HX[:, c, HALO + n * 128:HALO + (n + 1) * 128], Wkv[:, c, :], c == 0, c == 7, ["HXa", "WBk"], [("ps", pb)], rawonly=["HX"])
                    ACT(VL[:, n, :], PS[pb][:, 256:512], AF.Copy, [("ps", pb)], kVL)
                    yield from normrope_gen(PS[pb][:, 0:256], ("ps", pb), 2, QKG[:, 128:256], XCB[k][:, 0:256], ("XCB", k), k)
                    pk = ("pst", k)
                    for h in range(2):
                        TR(PST[k][:, h * 128:(h + 1) * 128], XCB[k][:, h * 128:(h + 1) * 128], [("XCB", k)], [pk])
                    P.op("act", lambda e: e.activation(out=KLT[:, :, n * 128:(n + 1) * 128], in_=PST[k][:, 0:256].rearrange("p (h t) -> p h t", h=2), func=AF.Copy), [pk], kKLT)
                    yield
                return gen

            for tb in range(NTB):
                run_interleaved([ktile(n) for n in range(4 * tb, 4 * tb + 4)], KT_SETS,
                                bg=rmsnorm_gen(tb + 1) if tb + 1 < NTB else None)
            if STOP == 'A0':
                raise _Stop()
            DMA("sp", "kb", kb_d.rearrange("(h p) t -> p h t", p=128), KLT, kKLT, ["kb"])
            DMA("sp", "vb", vb_d.rearrange("(n p) f -> p n f", p=128), VL, kVL, ["vb"])
            if STOP == 'A2':
                P.final_wait("sp", ["kb", "vb"])
                raise _Stop()
            AG("agk", kb_d, kg_d, ["kb"], ["kg"])
            if STOP == 'A3':
                raise _Stop()
            AG("agv", vb_d, vg_d, ["vb"], ["vg"])
            if l > 0:
                DMA("sp", "XG", XG, xg_d.rearrange("(r c p) t -> p r c t", r=4, p=128), ["xg"], [tk(0)])
                for side, (mcol, src) in enumerate(((0, slice(HALO, 2 * HALO)), (4, slice(0, HALO)))):
                    dst = XH[:, :, side * HALO:(side + 1) * HALO]
                    TS("dve", dst, XG[:, 0, :, src], CST[:, mcol:mcol + 1], None, ALU.mult, ALU.bypass, [tk(0), "CST"], ["XH"])
                    for r in range(1, 4):
                        STT(dst, XG[:, r, :, src], CST[:, mcol + r:mcol + r + 1], dst, ALU.mult, ALU.add, [tk(0), "CST", "XH"], ["XH"])

            for c in range(8):
                t = T[c % 2]
                ACT(t[:, 0:2 * HALO], XH[:, c, :], AF.Square, ["XH"], [tk(c % 2)])
                MM(PS[5][:, 0:2 * HALO], ONF[:], t[:, 0:2 * HALO], c == 0, c == 7, ["ONF", tk(c % 2)], [("ps", 5)])
            ACT(RSH[:], PS[5][:, 0:2 * HALO], AF.Ln, [("ps", 5)], ["RSH"], scale=1.0 / 1024, bias=EPS)
            ACT(RSH[:], RSH[:], AF.Exp, ["RSH"], ["RSH"], scale=-0.5)
            for c in range(8):
                STT(HX[:, c, 0:HALO], XH[:, c, 0:HALO], VEC[:, c:c + 1], RSH[:, 0:HALO], ALU.mult, ALU.mult, ["XH", "VEC", "RSH"], ["HX", "HXa"])
                STT(HX[:, c, NT + HALO:NT + 2 * HALO], XH[:, c, HALO:2 * HALO], VEC[:, c:c + 1], RSH[:, HALO:2 * HALO], ALU.mult, ALU.mult, ["XH", "VEC", "RSH"], ["HX", "HXa"])

            if STOP == 'A':
                raise _Stop()
            DMA("pool", "WB", Wl[:, 0:3, :], w_in_d[l, 0:384, 1024:2560].rearrange("(c p) e -> p c e", p=128), [], ["WB", "WBk"])
            ptkeys = lambda i: [tk(i)] + ([("PT", q) for q in range(NPT) if PTT[q // 2] == i])

            def lru_unit(tb, j):
                def gen(k):
                    s = HALO + tb * TB
                    blk = slice(tb * TB, (tb + 1) * TB)
                    tl = [T[7 * k + i] for i in range(7)]
                    kl = [ptkeys(7 * k + i) for i in range(7)]
                    UE, kUE = tl[0], kl[0]
                    XC, kXC = tl[1][:, 0:TB], kl[1]
                    t = lambda i: tl[2 + i][:, 0:TB]
                    kt_ = lambda i: kl[2 + i]
                    t4, k4 = tl[0][:, 0:TB], kl[0]
                    pA, pC = 2 * k, 2 * k + 1
                    cj = slice(j * 128, (j + 1) * 128)
                    zb = ZC[:, 0:1].to_broadcast([128, TB])
                    for c in range(8):
                        MM(PS[pA][:], Wl[:, c, cj], HX[:, c, s - 1:s + 511], c == 0, c == 7, ["WB", "HX"], [("ps", pA)])
                    for c in range(8):
                        MM(PS[pC][:, 0:3], Wl[:, c, cj], HX[:, c, s + 511:s + 514], c == 0, c == 7, ["WB", "HX"], [("ps", pC)])
                    yield
                    ACT(UE[:, 0:512], PS[pA][:], AF.Copy, [("ps", pA)], kUE)
                    ACT(UE[:, 512:515], PS[pC][:, 0:3], AF.Copy, [("ps", pC)], kUE)
                    for c in range(8):
                        MM(PS[pA][:], Wl[:, c, 768 + j * 128:768 + (j + 1) * 128], HX[:, c, s:s + TB], c == 0, c == 7, ["WB", "HX"], [("ps", pA)])
                    yield
                    cw = lambda q: VEC[:, 12 + q * 6 + j:13 + q * 6 + j]
                    TS("dve", XC, UE[:, 0:512], cw(0), VEC[:, 36 + j:37 + j], ALU.mult, ALU.add, [kUE[0], "VEC"], kXC)
                    STT(XC, UE[:, 1:513], cw(1), XC, ALU.mult, ALU.add, [kUE[0], "VEC", kXC[0]], [kXC[0]])
                    yield
                    STT(XC, UE[:, 2:514], cw(2), XC, ALU.mult, ALU.add, [kUE[0], "VEC", kXC[0]], [kXC[0]])
                    STT(XC, UE[:, 3:515], cw(3), XC, ALU.mult, ALU.add, [kUE[0], "VEC", kXC[0]], [kXC[0]])
                    ACT(XCB[k][:], XC, AF.Copy, [kXC[0]], [("XCB", k)])
                    yield
                    sigmoid_from(t4, k4[0], PS[pA][:], ("ps", pA), scale=-1.0)
                    TT("dve", t4, t4, PS[pA][:], ALU.mult, [k4[0], ("ps", pA)], [k4[0]])
                    yield
                    for d in range(2):
                        MM(PS[pC][:], WR[:, d, j, :], XCB[k][:], True, True, kWRI + [("XCB", k)], [("ps", pC)])
                        sigmoid_from(t(0), kt_(0)[0], PS[pC][:], ("ps", pC), scale=-1.0, bias=NB[:, d * 6 + j:d * 6 + j + 1])
                        MM(PS[pC][:], WI[:, d, j, :], XCB[k][:], True, True, kWRI + [("XCB", k)], [("ps", pC)])
                        yield
                        ta2 = t(2) if d == 0 else t(4)
                        ka2 = kt_(2) if d == 0 else kt_(4)
                        ACT(t(1), t(0), AF.Exp, [kt_(0)[0], "LSL"], kt_(1), scale=LSL[:, d * 6 + j:d * 6 + j + 1])
                        TT("dve", ta2, t(1), t(1), ALU.mult, [kt_(1)[0]], ka2)
                        yield
                        sigmoid_from(t(3), kt_(3)[0], PS[pC][:], ("ps", pC), scale=-1.0, bias=NB[:, 12 + d * 6 + j:12 + d * 6 + j + 1])
                        ACT(ta2, ta2, AF.Ln, [ka2[0]], [ka2[0]], scale=-1.0, bias=1.0 + 1.2e-7)
                        ACT(ta2, ta2, AF.Exp, [ka2[0]], [ka2[0]], scale=0.5)
                        yield
                        TT("dve", t(3), t(3), XC, ALU.mult, [kt_(3)[0], kXC[0]], [kt_(3)[0]])
                        TT("dve", t(3), t(3), ta2, ALU.mult, [kt_(3)[0], ka2[0]], [kt_(3)[0]])
                        yield
                        if d == 0:
                            first = tb == 0
                            SCAN(t(2), t(1), t(3), 0.0 if first else HS[:, j:j + 1], [kt_(1)[0], kt_(3)[0], "HS"], [kt_(2)[0]])
                            SCAN(t(0), t(1), zb, 1.0 if first else HS[:, 6 + j:7 + j], [kt_(1)[0], "ZC", "HS"], [kt_(0)[0]])
                            yield
                            P.op("pool", lambda e: e.tensor_copy(out=HS[:, j:j + 1], in_=tl[4][:, TB - 1:TB]), [kt_(2)[0]], ["HS"])
                            P.op("pool", lambda e: e.tensor_copy(out=HS[:, 6 + j:7 + j], in_=tl[2][:, TB - 1:TB]), [kt_(0)[0]], ["HS"])
                            TT("dve", CF[:, j, blk], t(0), t4, ALU.mult, [kt_(0)[0], k4[0]], kCF)
                            yield
                        else:
                            SCAN(t(4)[:, ::-1], t(1)[:, ::-1], t(3)[:, ::-1], 0.0, [kt_(1)[0], kt_(3)[0]], [kt_(4)[0]])
                            SCAN(t(0)[:, ::-1], t(1)[:, ::-1], zb, 1.0, [kt_(1)[0], "ZC"], [kt_(0)[0]])
                            yield
                            P.op("pool", lambda e: e.tensor_copy(out=EB[:, tb, j:j + 1], in_=tl[6][:, 0:1]), [kt_(4)[0]], ["EB"])
                            P.op("pool", lambda e: e.tensor_copy(out=PB[:, tb, j:j + 1], in_=tl[2][:, 0:1]), [kt_(0)[0]], ["PB"])
                            TT("dve", t(2), t(2), t(4), ALU.add, [kt_(2)[0], kt_(4)[0]], [kt_(2)[0]])
                            TT("dve", YB[:, j, :], t(2), t4, ALU.mult, [kt_(2)[0], k4[0]], [("YB", j)])
                            TT("dve", XCB[k][:], t(0), t4, ALU.mult, [kt_(0)[0], k4[0]], [("XCB", k)])
                            DMA("sp", "cbs", cb_d[:, j, blk], XCB[k][:], [("XCB", k)], [("cbd", tb)])
                            if j == 5:
                                outproj_add(WoL, "WB", 6, [("YB", jj) for jj in range(6)], tb, banks=(4, 5))
                            yield
                return gen

            run_interleaved([lru_unit(tb, j) for tb in range(NTB) for j in range(6)], 2)
            P.op("pool", lambda e: e.memset(HS[:, 12:18], 0.0), [], ["HSb"])
            P.op("pool", lambda e: e.memset(HS[:, 18:24], 1.0), [], ["HSb"])
            for tb in range(NTB - 1, -1, -1):
                TT("dve", HS[:, 12:18], HS[:, 12:18], PB[:, tb, :], ALU.mult, ["HSb", "PB"], ["HSb"])
                TT("dve", HS[:, 12:18], HS[:, 12:18], EB[:, tb, :], ALU.add, ["HSb", "EB"], ["HSb"])
                TT("dve", HS[:, 18:24], HS[:, 18:24], PB[:, tb, :], ALU.mult, ["HSb", "PB"], ["HSb"])
            DMA("sp", "sbn", sb_d, HS[:], ["HS", "HSb"], ["sbn"])
            AG("ags", sb_d, sg_d, ["sbn"], ["sgt"])
            Wp = wview(0, 8, 1024)
            PW = BIG[:, 15360:15872].rearrange("p (g e) -> p g e", g=4)
            WoP = WB[:, 8192:12288].rearrange("p (j m) -> p j m", j=4)
            DMA("pool", "WB", Wp, w_in_d[l, :, 0:1024].rearrange("(c p) e -> p c e", p=128), [], ["WB"])
            DMA("pool", "WRI", PW, pool_w_d[l].rearrange("g c e -> c g e"), [], kWRI)
            DMA("pool", "WB", WoP, w_out_d[l, 0:512, :].rearrange("(j p) m -> p j m", p=128), [], ["WB"])

            def pool_unit(tb, g):
                def gen(k):
                    w = POOL_W[g]
                    s = HALO + tb * TB
                    half = w // 2
                    cg = slice(g * 128, (g + 1) * 128)
                    UE, kUE = T[6 * k], tk(6 * k)
                    pA, pC = 2 * k, 2 * k + 1
                    for c in range(8):
                        MM(PS[pA][:], Wp[:, c, cg], HX[:, c, s - half:s - half + 512], c == 0, c == 7, ["WB", "HX"], [("ps", pA)])
                    for c in range(8):
                        MM(PS[pC][:, 0:w - 1], Wp[:, c, cg], HX[:, c, s - half + 512:s + 511 + half], c == 0, c == 7, ["WB", "HX"], [("ps", pC)])
                    yield
                    ACT(UE[:, 0:512], PS[pA][:], AF.Copy, [("ps", pA)], [kUE])
                    ACT(UE[:, 512:511 + w], PS[pC][:, 0:w - 1], AF.Copy, [("ps", pC)], [kUE])
                    for c in range(8):
                        MM(PS[pA][:], Wp[:, c, 512 + g * 128:512 + (g + 1) * 128], HX[:, c, s:s + TB], c == 0, c == 7, ["WB", "HX"], [("ps", pA)])
                    yield
                    L = 511 + w
                    src, sk = UE, kUE
                    step = 1
                    i = 0
                    while step < w:
                        L -= step
                        dst, dk = T[6 * k + 1 + i % 2], tk(6 * k + 1 + i % 2)
                        wk_ = [dk] + ([("PT", q) for q in range(NPT)] if 6 * k + 1 + i % 2 == 7 else [])
                        TT("dve", dst[:, 0:L], src[:, 0:L], src[:, step:step + L], ALU.add, [sk], wk_)
                        src, sk = dst, dk
                        step *= 2
                        i += 1
                        yield
                    if tb == 0:
                        TT("dve", src[:, 0:8], src[:, 0:8], CST[:, 16 + g * 16:24 + g * 16], ALU.mult, [sk, "CST"], [sk])
                    if tb == NTB - 1:
                        TT("dve", src[:, 504:512], src[:, 504:512], CST[:, 24 + g * 16:32 + g * 16], ALU.mult, [sk, "CST"], [sk])
                    STT(XCB[k][:], src[:, 0:512], 1.0 / w, UE[:, half:half + 512], ALU.mult, ALU.subtract, [sk, kUE], [("XCB", k)])
                    t3, k3 = T[6 * k + 3][:, 0:TB], tk(6 * k + 3)
                    sigmoid_from(t3, k3, PS[pA][:], ("ps", pA), scale=-1.0)
                    yield
                    MM(PS[pC][:], PW[:, g, :], XCB[k][:], True, True, kWRI + [("XCB", k)], [("ps", pC)])
                    TT("dve", t3, t3, PS[pA][:], ALU.mult, [k3, ("ps", pA)], [k3])
                    yield
                    STT(YB[:, g, :], PS[pC][:], VEC[:, 8 + g:9 + g], t3, ALU.mult, ALU.mult, [("ps", pC), "VEC", k3], [("YB", g)])
                    if g == 3:
                        outproj_add(WoP, "WB", 4, [("YB", jj) for jj in range(4)], tb, banks=(4, 5))
                    yield
                return gen

            run_interleaved([pool_unit(tb, g) for tb in range(NTB) for g in range(4)], 2)

            if STOP == 'pool':
                raise _Stop()
            DMA("sp", "SG4", SG4[:], sg_d.rearrange("(r p) f -> p r f", p=128), ["sgt"], ["SG4"])
            for d in range(2):
                order = (0, 1, 2) if d == 0 else (3, 2, 1)
                co = 12 * d
                dst = CARRY if d == 0 else CARB
                dk = "CARRY" if d == 0 else "CARB"
                P.op("pool", lambda e: e.memset(CAR[0][:], 0.0), [], ["CAR0"])
                P.op("pool", lambda e, dst=dst: e.memset(dst[:], 0.0), [], [dk])
                for i, kk in enumerate(order):
                    TT("dve", CAR[i + 1][:], SG4[:, kk, co + 6:co + 12], CAR[i][:], ALU.mult, ["SG4", "CAR%d" % i], ["CAR%d" % (i + 1)])
                    TT("dve", CAR[i + 1][:], CAR[i + 1][:], SG4[:, kk, co:co + 6], ALU.add, ["SG4", "CAR%d" % (i + 1)], ["CAR%d" % (i + 1)])
                    knext = kk + 1 if d == 0 else kk - 1
                    STT(dst[:], CAR[i + 1][:], CST[:, 8 + knext:9 + knext], dst[:], ALU.mult, ALU.add, ["CAR%d" % (i + 1), "CST", dk], [dk])
            P.op("dve", lambda e: e.tensor_copy(out=HIN[:, NTB - 1, :], in_=CARB[:]), ["CARB"], ["HIN"])
            for tb in range(NTB - 2, -1, -1):
                TT("dve", HIN[:, tb, :], HIN[:, tb + 1, :], PB[:, tb + 1, :], ALU.mult, ["HIN", "PB"], ["HIN"])
                TT("dve", HIN[:, tb, :], HIN[:, tb, :], EB[:, tb + 1, :], ALU.add, ["HIN", "EB"], ["HIN"])
            for tb in range(NTB):
                blk = slice(tb * TB, (tb + 1) * TB)
                cbs = []
                for q in range(3):
                    v = T[q][:, 0:TB].bitcast(BF16)
                    cbs += [v[:, 0:TB], v[:, TB:2 * TB]]
                for q in range(3):
                    DMA("sp", "CBS%d" % q, T[q][:, 0:TB].bitcast(BF16).rearrange("p (j t) -> p j t", j=2), cb_d[:, 2 * q:2 * q + 2, blk], [("cbd", tb)], [tk(q)])
                for j in range(6):
                    TS("dve", YB[:, j, :], CF[:, j, blk], CARRY[:, j:j + 1], None, ALU.mult, ALU.bypass, kCF + ["CARRY"], [("YB", j)])
                    STT(YB[:, j, :], cbs[j], HIN[:, tb, j:j + 1], YB[:, j, :], ALU.mult, ALU.add, [tk(j // 2), "HIN", ("YB", j)], [("YB", j)])
                outproj_add(WoL, "WB", 6, [("YB", jj) for jj in range(6)], tb, banks=(4, 5))

            DMA("sp", "KT", KT, kg_d.rearrange("(r h p) t -> p r h t", h=2, p=128)[:, :, 0, :], ["kg"], kKT)
            DMA("sp", "VA", VA, vg_d.rearrange("(n p) f -> p n f", p=128)[:, :, 0:128], ["vg"], kVA)
            scl = 128.0 ** -0.5
            ACC = (T[8][:, 0:TB], T[9][:, 0:TB])
            for kvh in range(2):
                if kvh == 1:
                    DMA("sp", "KT", KT, kg_d.rearrange("(r h p) t -> p r h t", h=2, p=128)[:, :, kvh, :], ["kg"], kKT)
                    DMA("sp", "VA", VA, vg_d.rearrange("(n p) f -> p n f", p=128)[:, :, kvh * 128:(kvh + 1) * 128], ["vg"], kVA)
                wo = kvh * 9216
                wkey = "WBa%d" % kvh
                Wq = wview(wo, 8, 384)
                Wg = wview(wo + 3072, 8, 384)
                WoA = WB[:, wo + 6144:wo + 9216].rearrange("p (j m) -> p j m", j=3)

                def load_attn_w(kv):
                    o2 = kv * 9216
                    kk = ["WBa%d" % kv, "WB"]
                    DMA("pool", "WBa%d" % kv, wview(o2, 8, 384), w_in_d[l, :, 2560 + kv * 384:2560 + (kv + 1) * 384].rearrange("(c p) e -> p c e", p=128), [], kk)
                    DMA("pool", "WBa%d" % kv, wview(o2 + 3072, 8, 384), w_in_d[l, :, 3840 + kv * 384:3840 + (kv + 1) * 384].rearrange("(c p) e -> p c e", p=128), [], kk)
                    DMA("pool", "WBa%d" % kv, WB[:, o2 + 6144:o2 + 9216].rearrange("p (j m) -> p j m", j=3), w_out_d[l, 1280 + kv * 384:1280 + (kv + 1) * 384, :].rearrange("(j p) m -> p j m", p=128), [], kk)

                if kvh == 0:
                    load_attn_w(0)
                    load_attn_w(1)
                cs_begin([tb * 4 + n4 for tb in range(NTB) for n4 in range(4)])
                T6v = T[6][:, 0:TB].bitcast(BF16)
                T9v = T[9][:, 0:TB].bitcast(BF16)
                T12v = T[12][:, 0:TB].bitcast(BF16)
                T13v = T[13][:, 0:TB].bitcast(BF16)
                QTB = [[T12v[:, 0:TB], T12v[:, TB:2 * TB], T13v[:, 0:TB]], [T6v[:, 0:TB], T6v[:, TB:2 * TB], T9v[:, 0:TB]]]
                qtw = [[("QT", 0), tk(12), tk(13)], [("QT", 1), tk(6), tk(9)]]

                def qtile(tb, n4, buf):
                    def gen(k):
                        s = HALO + tb * TB
                        pb = 3 + k
                        for c in range(8):
                            MM(PS[pb][:, 0:384], HX[:, c, s + n4 * 128:s + (n4 + 1) * 128], Wq[:, c, :], c == 0, c == 7, ["HX", wkey], [("ps", pb)])
                        yield from normrope_gen(PS[pb][:, 0:384], ("ps", pb), 3, QKG[:, 0:128], XCB[k][:, 0:384], ("XCB", k), k)
                        for hh in range(3):
                            TR(PST[k][:, hh * 128:(hh + 1) * 128], XCB[k][:, hh * 128:(hh + 1) * 128], [("XCB", k)], [("pst", k)])
                        yield
                        for hh in range(3):
                            ACT(QTB[buf][hh][:, n4 * 128:(n4 + 1) * 128], PST[k][:, hh * 128:(hh + 1) * 128], AF.Copy, [("pst", k)], qtw[buf])
                        yield
                    return gen

                def qprep_inloop(tb, buf):
                    for n4 in range(4):
                        yield from qtile(tb, n4, buf)(0)

                run_interleaved([qtile(0, n4, 0) for n4 in range(4)], 2)
                for tb in range(NTB):
                    s = HALO + tb * TB
                    buf = tb % 2
                    prep = qprep_inloop(tb + 1, 1 - buf) if tb + 1 < NTB else None
                    if not QPREP_INLOOP and prep is not None:
                        for _ in prep:
                            pass
                        prep = None
                    it = 0
                    for hh in range(3):
                        for c in range(8):
                            MM(PS[3][:], Wg[:, c, hh * 128:(hh + 1) * 128], HX[:, c, s:s + TB], c == 0, c == 7, [wkey, "HX"], [("ps", 3)])
                        t5, k5 = T[5][:, 0:TB], tk(5)
                        t4, k4 = T[4][:, 0:TB], tk(4)
                        sigmoid_from(t5, k5, PS[3][:], ("ps", 3), scale=-1.0)
                        TT("dve", t5, t5, PS[3][:], ALU.mult, [k5, ("ps", 3)], [k5])

                        def QK(kt):
                            b = kt % 3
                            MM(PS[b][:], KT[:, kt // 16, (kt % 16) * 128:(kt % 16 + 1) * 128], QTB[buf][hh], True, True, kKT + [("QT", buf)], [("ps", b)])
                        QK(0)
                        QK(1)
                        def consume(kt):
                            pb = kt % NPT
                            MM(PS[4][:], VA[:, kt, :], PT[pb], kt == 0, kt == 63, kVA + [("PT", pb)], [("ps", 4)])
                            if kt % 3 == 2:
                                MM(PS[5][:], ONB[:], PT[pb], kt == 2, False, ["ONB", ("PT", pb)], [("ps", 5)])
                            elif kt == 0:
                                P.op("dve", lambda e, pb=pb: e.tensor_copy(out=ACC[0], in_=PT[pb]), [("PT", pb)], [tk(8)])
                            else:
                                TT("dve", ACC[0], ACC[0], PT[pb], ALU.add, [tk(8), ("PT", pb)], [tk(8)])

                        for kt in range(64):
                            b = kt % 3
                            pb = kt % NPT
                            ACT(PT[pb], PS[b][:], AF.Exp, [("ps", b)], [("PT", pb), ptk(pb)], scale=scl)
                            if kt + 2 < 64:
                                QK(kt + 2)
                            if kt >= 1:
                                consume(kt - 1)
                            it += 1
                            if prep is not None and it % 6 == 0:
                                try:
                                    next(prep)
                                except StopIteration:
                                    prep = None
                        consume(63)
                        MM(PS[5][:], ONF[:], ACC[0], False, True, ["ONF", tk(8)], [("ps", 5)])
                        ACT(t4, PS[5][:], AF.Ln, [("ps", 5)], [k4])
                        ACT(t4, t4, AF.Exp, [k4], [k4], scale=-1.0)
                        TT("dve", t4, t4, t5, ALU.mult, [k4, k5], [k4])
                        TT("dve", YB[:, hh, :], PS[4][:], t4, ALU.mult, [("ps", 4), k4], [("YB", hh)])
                    if prep is not None:
                        for _ in prep:
                            pass
                    outproj_add(WoA, wkey, 3, [("YB", jj) for jj in range(3)], tb, banks=(3, 5))

        try:
            for l in range(nl):
                layer(l)
        except _Stop:
            pass
        DMA("sp", "yT", yT_d.rearrange("(c p) t -> p c t", p=128), X[:], [("X", m) for m in range(8)], ["yT"])
        P.final_wait("sp", ["yT", "kg", "vg", "sgt", "xg"])
        print("SBUF remaining bytes/partition:", nc.sbuf_bytes_remaining)
        P.emit()
    return nc


_NC_CACHE = {}


def _get_nc(nl):
    if nl not in _NC_CACHE:
        _NC_CACHE[nl] = build(nl)
    return _NC_CACHE[nl]


def _host_consts():
    inv = (10000.0 ** (-np.arange(0, 64, 2, dtype=np.float32) / 64.0)).astype(np.float32)
    out = []
    for c in range(NCORES):
        r = c % 4
        t = np.arange(r * NT, (r + 1) * NT)
        row = (t // 64).astype(np.float32)
        col = (t % 64).astype(np.float32)
        ang = np.concatenate([row[:, None] * inv[None, :], col[:, None] * inv[None, :]], axis=1).astype(np.float32)
        cs = np.concatenate([np.cos(ang), np.sin(ang)], axis=1).astype(np.float32)
        cst = np.zeros((80,), np.float32)
        if r > 0:
            cst[r - 1] = 1.0
        if r < 3:
            cst[4 + r + 1] = 1.0
        cst[8 + r] = 1.0
        L = 4 * NT
        for g, w in enumerate(POOL_W):
            half = w // 2
            fl = np.ones(8, np.float32)
            fr = np.ones(8, np.float32)
            if r == 0:
                tt = np.arange(0, 8)
                cnt = np.clip(tt + half, 0, L) - np.clip(tt - half, 0, L)
                fl = (w / cnt).astype(np.float32)
            if r == 3:
                tt = np.arange(L - 8, L)
                cnt = np.clip(tt + half, 0, L) - np.clip(tt - half, 0, L)
                fr = (w / cnt).astype(np.float32)
            cst[16 + g * 16:24 + g * 16] = fl
            cst[24 + g * 16:32 + g * 16] = fr
        out.append((cs, np.ascontiguousarray(np.broadcast_to(cst[None, :], (128, 80)))))
    return out


def _pack_vecs(norm_g, pool_scale, conv_w, conv_b, lru_br, lru_bi, lru_lam):
    nl = norm_g.shape[0]
    v = np.zeros((nl, 128, NV), np.float32)
    fm = lambda a, n: a.reshape(n, 128).T
    for l in range(nl):
        v[l, :, 0:8] = fm(norm_g[l], 8)
        v[l, :, 8:12] = fm(pool_scale[l], 4)
        for k in range(4):
            v[l, :, 12 + k * 6:18 + k * 6] = fm(conv_w[l, k], 6)
        v[l, :, 36:42] = fm(conv_b[l], 6)
        for d in range(2):
            v[l, :, 42 + d * 6:48 + d * 6] = fm(lru_br[l, d], 6)
            v[l, :, 54 + d * 6:60 + d * 6] = fm(lru_bi[l, d], 6)
            v[l, :, 66 + d * 6:72 + d * 6] = fm(lru_lam[l, d], 6)
    return v


LAYERS_PER_LAUNCH = 4
KT_SETS = 2
QPREP_INLOOP = True
PST_SPLIT = 512
LRU_SETS = 2
STOP = None


def kernel(x, norm_g, w_in, pool_w, pool_scale, conv_w, conv_b, lru_wr, lru_br, lru_wi, lru_bi,
           lru_lam, q_norm, k_norm, w_out):
    f = lambda a: np.ascontiguousarray(np.asarray(a, dtype=np.float32))
    x = f(x)
    vecs = _pack_vecs(f(norm_g), f(pool_scale), f(conv_w), f(conv_b), f(lru_br), f(lru_bi), f(lru_lam))
    qkn = np.ascontiguousarray(np.concatenate([f(q_norm), f(k_norm)], axis=1)[:, None, :])
    w_in, w_out, pool_w, lru_wr, lru_wi = f(w_in), f(w_out), f(pool_w), f(lru_wr), f(lru_wi)
    consts = _host_consts()
    nlp = LAYERS_PER_LAUNCH
    nc = _get_nc(nlp)
    cur = x
    for l0 in range(0, DEPTH, nlp):
        ls = slice(l0, l0 + nlp)
        in_maps = []
        for c in range(NCORES):
            b, r = c // 4, c % 4
            xs = cur[b, r * NT:(r + 1) * NT, :]
            xh = np.zeros((2 * HALO, 1024), np.float32)
            if r > 0:
                xh[0:HALO] = cur[b, r * NT - HALO:r * NT, :]
            if r < 3:
                xh[HALO:] = cur[b, (r + 1) * NT:(r + 1) * NT + HALO, :]
            in_maps.append({
                "xT": np.ascontiguousarray(xs.T), "xh": np.ascontiguousarray(xh.T),
                "cs": consts[c][0], "cst": consts[c][1],
                "vecs": np.ascontiguousarray(vecs[ls]), "qkn": np.ascontiguousarray(qkn[ls]),
                "w_in": w_in[ls], "w_out": w_out[ls], "pool_w": pool_w[ls],
                "lru_wr": lru_wr[ls], "lru_wi": lru_wi[ls],
            })
        res = run_bass_kernel_spmd(nc, in_maps, core_ids=list(range(NCORES)))
        nxt = np.empty_like(cur)
        for c in range(NCORES):
            b, r = c // 4, c % 4
            nxt[b, r * NT:(r + 1) * NT, :] = np.asarray(res.results[c]["yT"]).T
        cur = nxt
    return cur
```
